# Optimizing a Trainium2 kernel written in Bass

```python
import math
import jax, jax.numpy as jnp
from jax import lax
import numpy as np

D_MODEL = 1024
BATCH = 16
SEQ = 256
DEPTH = 2
DEC_BATCH = 8
DEC_SEQ = 1024
PAST_LEN = 512

GRID_W = 64
D_MIX = 2 * D_MODEL
A_INNER = D_MIX // 2
A_HEAD_DIM = 64
A_HEADS = A_INNER // A_HEAD_DIM
A_GROUPS = 4
A_HPG = A_HEADS // A_GROUPS
A_D_STATE = 128
A_GN = A_GROUPS * A_D_STATE
A_CONV = 4
A_CONV_CH = A_INNER + 2 * A_GN
A_CHUNK = 128
A_IN = A_INNER + A_CONV_CH + 2 * A_HEADS
B_CH = D_MIX // 2
B_CONV = 31
B_IN = 3 * B_CH
L0_IN = A_IN + B_IN
L0_OUT = A_INNER + B_CH
C_WIDTH = D_MIX
C_GROUPS = 16
C_GROUP_DIM = C_WIDTH // C_GROUPS
C_CHUNK = 128
L1_IN = 3 * C_WIDTH
ALPHA = (2 * DEPTH) ** 0.25
BETA = (8 * DEPTH) ** -0.25
LN_EPS = 1e-5
RMS_EPS = 1e-5
POS_BASE = 10000.0

kernel_name = "hybrid_ssd_conformer_gmlp_diffusion_step"


def layer_norm(x, g, b):
    xf = x.astype(jnp.float32)
    mu = jnp.mean(xf, axis=-1, keepdims=True)
    var = jnp.mean(jnp.square(xf - mu), axis=-1, keepdims=True)
    out = (xf - mu) * lax.rsqrt(var + LN_EPS) * g.astype(jnp.float32) + b.astype(jnp.float32)
    return out.astype(x.dtype)


def depthwise_conv(x, w, bias, pad_lo, pad_hi):
    C = w.shape[1]
    out = lax.conv_general_dilated(
        x, w[:, None, :].astype(x.dtype), window_strides=(1,),
        padding=[(pad_lo, pad_hi)], dimension_numbers=("NWC", "WIO", "NWC"),
        feature_group_count=C)
    return out + bias.astype(x.dtype)


def adaln(cond, w, b):
    m = (jax.nn.silu(cond) @ w + b)[:, None, :]
    return jnp.split(m, 3, axis=-1)


def grid_pos_embed(L, d):
    rows = L // GRID_W
    quarter = d // 4
    freqs = 1.0 / (POS_BASE ** (jnp.arange(quarter, dtype=jnp.float32) / quarter))
    r = jnp.arange(rows, dtype=jnp.float32)[:, None] * freqs
    cc = jnp.arange(GRID_W, dtype=jnp.float32)[:, None] * freqs
    emb_r = jnp.concatenate([jnp.sin(r), jnp.cos(r)], axis=-1)
    emb_c = jnp.concatenate([jnp.sin(cc), jnp.cos(cc)], axis=-1)
    emb = jnp.concatenate([
        jnp.broadcast_to(emb_r[:, None, :], (rows, GRID_W, d // 2)),
        jnp.broadcast_to(emb_c[None, :, :], (rows, GRID_W, d // 2))], axis=-1)
    return emb.reshape(rows * GRID_W, d)


def ssd_chunked_scan(x, dt, a_neg, bm, cm, h0):
    b, L, G, R, P = x.shape
    N = bm.shape[-1]
    nc = L // A_CHUNK
    x = x.reshape(b, nc, A_CHUNK, G, R, P)
    dt = dt.reshape(b, nc, A_CHUNK, G, R)
    bm = bm.reshape(b, nc, A_CHUNK, G, N)
    cm = cm.reshape(b, nc, A_CHUNK, G, N)
    acs = jnp.cumsum(dt * a_neg, axis=2)
    tril = jnp.tril(jnp.ones((A_CHUNK, A_CHUNK), dtype=bool))
    seg = acs[:, :, :, None] - acs[:, :, None, :]
    lmat = jnp.exp(jnp.where(tril[:, :, None, None], seg, -jnp.inf))
    xdt = x * dt[..., None]
    cb = jnp.einsum("bcign,bcjgn->bcijg", cm, bm)
    y_diag = jnp.einsum("bcijg,bcijgr,bcjgrp->bcigrp", cb, lmat, xdt)
    decay_end = jnp.exp(acs[:, :, -1:] - acs)
    chunk_states = jnp.einsum("bcjgn,bcjgr,bcjgrp->bcgrpn", bm, decay_end, xdt)
    chunk_decay = jnp.exp(acs[:, :, -1])

    def step(h, inp):
        dec, st = inp
        return h * dec[..., None, None] + st, h

    h_final, h_in = lax.scan(step, h0, (jnp.moveaxis(chunk_decay, 1, 0), jnp.moveaxis(chunk_states, 1, 0)))
    h_in = jnp.moveaxis(h_in, 0, 1)
    y_off = jnp.einsum("bcign,bcigr,bcgrpn->bcigrp", cm, jnp.exp(acs), h_in)
    return (y_diag + y_off).reshape(b, L, G, R, P), h_final


def ssd_mixer(pa, h0f, h0b, p):
    b, L, _ = pa.shape
    f32 = jnp.float32
    z = pa[..., :A_INNER]
    xbc = pa[..., A_INNER:A_INNER + A_CONV_CH]
    dt_raw = pa[..., A_INNER + A_CONV_CH:].astype(f32)
    xbc = jax.nn.silu(depthwise_conv(xbc, p["a_conv_w"], p["a_conv_b"], A_CONV // 2, A_CONV - 1 - A_CONV // 2)).astype(f32)
    xs = xbc[..., :A_INNER].reshape(b, L, A_GROUPS, A_HPG, A_HEAD_DIM)
    bm = xbc[..., A_INNER:A_INNER + A_GN].reshape(b, L, A_GROUPS, A_D_STATE)
    cm = xbc[..., A_INNER + A_GN:].reshape(b, L, A_GROUPS, A_D_STATE)
    dtf = jax.nn.softplus(dt_raw[..., :A_HEADS] + p["a_dt_bias_f"].astype(f32)).reshape(b, L, A_GROUPS, A_HPG)
    dtb = jax.nn.softplus(dt_raw[..., A_HEADS:] + p["a_dt_bias_b"].astype(f32)).reshape(b, L, A_GROUPS, A_HPG)
    af = -jnp.exp(p["a_log_f"].astype(f32)).reshape(A_GROUPS, A_HPG)
    ab = -jnp.exp(p["a_log_b"].astype(f32)).reshape(A_GROUPS, A_HPG)
    sshape = (b, A_GROUPS, A_HPG, A_HEAD_DIM, A_D_STATE)
    h0f = h0f.astype(f32).reshape(sshape)
    h0b = h0b.astype(f32).reshape(sshape)
    yf, hf = ssd_chunked_scan(xs, dtf, af, bm, cm, h0f)
    rev = lambda t: jnp.flip(t, axis=1)
    yb, hb = ssd_chunked_scan(rev(xs), rev(dtb), ab, rev(bm), rev(cm), h0b)
    y = yf + rev(yb) + p["a_d"].astype(f32).reshape(A_GROUPS, A_HPG)[:, :, None] * xs
    y = y.reshape(b, L, A_INNER) * jax.nn.silu(z.astype(f32))
    yg = y.reshape(b, L, A_GROUPS, A_INNER // A_GROUPS)
    yg = yg * lax.rsqrt(jnp.mean(jnp.square(yg), axis=-1, keepdims=True) + RMS_EPS)
    y = yg.reshape(b, L, A_INNER) * p["a_norm_w"].astype(f32)
    out_shape = (b, A_HEADS, A_HEAD_DIM, A_D_STATE)
    return y.astype(pa.dtype), hf.reshape(out_shape), hb.reshape(out_shape)


def conformer_conv_mixer(pb, p):
    val, glu_gate, gate = jnp.split(pb, 3, axis=-1)
    u = val * jax.nn.sigmoid(glu_gate)
    u = depthwise_conv(u, p["b_conv_w"], p["b_conv_b"], B_CONV // 2, B_CONV // 2)
    u = jax.nn.silu(layer_norm(u, p["b_ln_g"], p["b_ln_b"]))
    return u * jax.nn.silu(gate)


def chunk_gmlp_mixer(pc, p):
    u, v, gate = jnp.split(pc, 3, axis=-1)
    v = layer_norm(v, p["c_ln_g"], p["c_ln_b"])
    b, L, _ = v.shape
    vc = v.reshape(b, L // C_CHUNK, C_CHUNK, C_GROUPS, C_GROUP_DIM)
    s = jnp.einsum("gij,bcjgd->bcigd", p["c_ws"].astype(v.dtype), vc)
    s = s + p["c_bs"].T.astype(v.dtype)[:, :, None]
    s = s.reshape(b, L, C_WIDTH)
    return u * s * jax.nn.silu(gate)


def even_layer(x, cond, h0f, h0b, p):
    shift, scale, gate = adaln(cond, p["ada_w"], p["ada_b"])
    h = x * (1 + scale) + shift
    proj = h @ p["w_in"]
    ya, hf, hb = ssd_mixer(proj[..., :A_IN], h0f, h0b, p)
    yb = conformer_conv_mixer(proj[..., A_IN:], p)
    y = jnp.concatenate([ya, yb], axis=-1) @ p["w_out"]
    x = layer_norm(ALPHA * x + gate * y, p["ln_g"], p["ln_b"])
    return x, hf, hb


def odd_layer(x, cond, p):
    shift, scale, gate = adaln(cond, p["ada_w"], p["ada_b"])
    h = x * (1 + scale) + shift
    y = chunk_gmlp_mixer(h @ p["w_in"], p) @ p["w_out"]
    return layer_norm(ALPHA * x + gate * y, p["ln_g"], p["ln_b"])


def setup_inputs(seed: int = 0) -> dict:
    key = jax.random.key(seed)
    ks = iter(jax.random.split(key, 48))
    nrm = lambda shape, scale: jax.random.normal(next(ks), shape, jnp.float32) * scale

    def dt_bias():
        dt = jnp.exp(jax.random.uniform(next(ks), (A_HEADS,), jnp.float32, math.log(1e-3), math.log(1e-1)))
        return dt + jnp.log(-jnp.expm1(-dt))

    def a_log():
        return jnp.log(jax.random.uniform(next(ks), (A_HEADS,), jnp.float32, 1.0, 16.0))

    d_in = D_MODEL ** -0.5
    return {
        "x_prompt": nrm((BATCH, SEQ, D_MODEL), 1.0),
        "x_sample": nrm((DEC_BATCH, DEC_SEQ, D_MODEL), 1.0),
        "state_ssd_fwd_l0": nrm((DEC_BATCH, A_HEADS, A_HEAD_DIM, A_D_STATE), 0.5),
        "state_ssd_bwd_l0": nrm((DEC_BATCH, A_HEADS, A_HEAD_DIM, A_D_STATE), 0.5),
        "c": nrm((DEC_BATCH, D_MODEL), 1.0),
        "c_ctx": nrm((D_MODEL,), 1.0),
        "ada_w_l0": nrm((D_MODEL, 3 * D_MODEL), d_in),
        "ada_b_l0": nrm((3 * D_MODEL,), 0.02),
        "w_in_l0": nrm((D_MODEL, L0_IN), d_in),
        "a_conv_w_l0": nrm((A_CONV, A_CONV_CH), A_CONV ** -0.5),
        "a_conv_b_l0": nrm((A_CONV_CH,), 0.02),
        "a_dt_bias_f_l0": dt_bias(),
        "a_dt_bias_b_l0": dt_bias(),
        "a_log_f_l0": a_log(),
        "a_log_b_l0": a_log(),
        "a_d_l0": 1.0 + nrm((A_HEADS,), 0.02),
        "a_norm_w_l0": 1.0 + nrm((A_INNER,), 0.02),
        "b_conv_w_l0": nrm((B_CONV, B_CH), B_CONV ** -0.5),
        "b_conv_b_l0": nrm((B_CH,), 0.02),
        "b_ln_g_l0": 1.0 + nrm((B_CH,), 0.02),
        "b_ln_b_l0": nrm((B_CH,), 0.02),
        "w_out_l0": nrm((L0_OUT, D_MODEL), L0_OUT ** -0.5 * BETA),
        "ln_g_l0": 1.0 + nrm((D_MODEL,), 0.02),
        "ln_b_l0": nrm((D_MODEL,), 0.02),
        "ada_w_l1": nrm((D_MODEL, 3 * D_MODEL), d_in),
        "ada_b_l1": nrm((3 * D_MODEL,), 0.02),
        "w_in_l1": nrm((D_MODEL, L1_IN), d_in),
        "c_ln_g_l1": 1.0 + nrm((C_WIDTH,), 0.02),
        "c_ln_b_l1": nrm((C_WIDTH,), 0.02),
        "c_ws_l1": nrm((C_GROUPS, C_CHUNK, C_CHUNK), C_CHUNK ** -0.5),
        "c_bs_l1": 1.0 + nrm((C_GROUPS, C_CHUNK), 0.02),
        "w_out_l1": nrm((C_WIDTH, D_MODEL), C_WIDTH ** -0.5 * BETA),
        "ln_g_l1": 1.0 + nrm((D_MODEL,), 0.02),
        "ln_b_l1": nrm((D_MODEL,), 0.02),
    }


def reference(x_prompt, x_sample, state_ssd_fwd_l0, state_ssd_bwd_l0, c, c_ctx,
              ada_w_l0, ada_b_l0, w_in_l0, a_conv_w_l0, a_conv_b_l0,
              a_dt_bias_f_l0, a_dt_bias_b_l0, a_log_f_l0, a_log_b_l0, a_d_l0, a_norm_w_l0,
              b_conv_w_l0, b_conv_b_l0, b_ln_g_l0, b_ln_b_l0, w_out_l0, ln_g_l0, ln_b_l0,
              ada_w_l1, ada_b_l1, w_in_l1, c_ln_g_l1, c_ln_b_l1, c_ws_l1, c_bs_l1,
              w_out_l1, ln_g_l1, ln_b_l1):
    layers = [
        dict(ada_w=ada_w_l0, ada_b=ada_b_l0, w_in=w_in_l0, a_conv_w=a_conv_w_l0, a_conv_b=a_conv_b_l0,
             a_dt_bias_f=a_dt_bias_f_l0, a_dt_bias_b=a_dt_bias_b_l0, a_log_f=a_log_f_l0, a_log_b=a_log_b_l0,
             a_d=a_d_l0, a_norm_w=a_norm_w_l0, b_conv_w=b_conv_w_l0, b_conv_b=b_conv_b_l0,
             b_ln_g=b_ln_g_l0, b_ln_b=b_ln_b_l0, w_out=w_out_l0, ln_g=ln_g_l0, ln_b=ln_b_l0),
        dict(ada_w=ada_w_l1, ada_b=ada_b_l1, w_in=w_in_l1, c_ln_g=c_ln_g_l1, c_ln_b=c_ln_b_l1,
             c_ws=c_ws_l1, c_bs=c_bs_l1, w_out=w_out_l1, ln_g=ln_g_l1, ln_b=ln_b_l1),
    ]
    cache_states = [(state_ssd_fwd_l0, state_ssd_bwd_l0)]
    cond_ctx = c_ctx[None, :]
    x_p = x_prompt
    x_s = x_sample + grid_pos_embed(x_sample.shape[1], D_MODEL).astype(x_sample.dtype)
    new_states = []
    for i in range(DEPTH):
        p = layers[i]
        if i % 2 == 0:
            zeros = jnp.zeros((x_p.shape[0], A_HEADS, A_HEAD_DIM, A_D_STATE), jnp.float32)
            x_p, hf, hb = even_layer(x_p, cond_ctx, zeros, zeros, p)
            new_states.append((hf, hb))
            cf, cb = cache_states[i // 2]
            x_s, _, _ = even_layer(x_s, c, cf, cb, p)
        else:
            x_p = odd_layer(x_p, cond_ctx, p)
            x_s = odd_layer(x_s, c, p)
    new_ssd_fwd_l0, new_ssd_bwd_l0 = new_states[0]
    return (x_p, x_s, new_ssd_fwd_l0, new_ssd_bwd_l0)
```

```python
import math
import numpy as np
from contextlib import ExitStack
import concourse.bass as bass
import concourse.mybir as mybir
from concourse.bass_utils import run_bass_kernel_spmd

F32 = mybir.dt.float32
BF16 = mybir.dt.bfloat16
I32 = mybir.dt.int32
ALU = mybir.AluOpType
AF = mybir.ActivationFunctionType
AX = mybir.AxisListType

D = 1024
NT = 12
NTOK = 1536
ALPHA = 4.0 ** 0.25
LN_EPS = 1e-5
RMS_EPS = 1e-5
TWO_PI = 2.0 * math.pi
SEQS = [(0, 2), (2, 2), (4, 8)]
UOFF = [15, 301, 587]
UL = 1626
ROFF = [2, 261, 520]
RL = 1545


def cond_of(t):
    return 0 if t < 4 else 1


class _Rec:
    def __init__(self):
        self.calls = []

    def __getattr__(self, name):
        def m(*a, **k):
            self.calls.append((name, a, k))
            return None
        return m


class KB:
    ENG = ("pe", "act", "dve", "pool", "sp")

    def __init__(self, nc, es):
        self.nc = nc
        self.es = es
        self.sem = {e: es.enter_context(nc.semaphore("s_" + e)) for e in self.ENG}
        self.cnt = {e: 0 for e in self.ENG}
        self.known = {e: {} for e in self.ENG}
        self.streams = {e: [] for e in self.ENG}
        self.dsem = {}
        self.dcnt = {}
        self.resw = {}
        self.resr = {}

    def _deps(self, reads, writes):
        deps = []
        for r in reads:
            t = self.resw.get(r)
            if t is not None:
                deps.append(t)
        for w in writes:
            t = self.resw.get(w)
            if t is not None:
                deps.append(t)
            deps.extend(self.resr.get(w, ()))
        return deps

    def _waits(self, eng, deps):
        kn = self.known[eng]
        best = {}
        for (sk, val, clock) in deps:
            if kn.get(sk, 0) >= val:
                continue
            if best.get(sk, 0) < val:
                best[sk] = val
            for k2, v2 in clock.items():
                if kn.get(k2, 0) < v2:
                    kn[k2] = v2
            kn[sk] = val
        return list(best.items())

    def _commit(self, token, reads, writes):
        for w in writes:
            self.resw[w] = token
            self.resr[w] = []
        for r in reads:
            if r in writes:
                continue
            self.resr.setdefault(r, []).append(token)

    def emit(self, eng, fn, reads=(), writes=()):
        pr = [r for r in reads if (r == "ps7" or (isinstance(r, tuple) and r[0] == "ps")) and r not in writes]
        deps = self._deps([r for r in reads if r not in pr], writes)
        for r in pr:
            lw = self.resw.get(r)
            if lw is not None:
                deps.append(lw)
            for t in self.resr.get(r, ()):
                if t[0] != eng:
                    deps.append(t)
        waits = self._waits(eng, deps)
        self.cnt[eng] += 1
        val = self.cnt[eng]
        token = (eng, val, dict(self.known[eng]))
        if eng == "pe":
            self.known[eng][eng] = val
        rec = _Rec()
        fn(rec)
        calls = rec.calls
        assert calls

        def fn2(e, calls=calls):
            for name, a, k in calls:
                ins = getattr(e, name)(*a, **k)
            return ins
        self.streams[eng].append((waits, fn2, (eng, 1)))
        self._commit(token, reads, writes)
        return token

    def dma(self, eng, semkey, out, in_, reads=(), writes=()):
        if semkey not in self.dsem:
            self.dsem[semkey] = self.es.enter_context(self.nc.semaphore("d_" + semkey))
            self.dcnt[semkey] = 0
        waits = self._waits(eng, self._deps(reads, writes))
        self.dcnt[semkey] += 16
        val = self.dcnt[semkey]
        token = ("D:" + semkey, val, dict(self.known[eng]))

        def fn(e, out=out, in_=in_):
            return e.dma_start(out=out, in_=in_)
        self.streams[eng].append((waits, fn, ("D:" + semkey, 16)))
        self._commit(token, reads, writes)
        return token

    def semh(self, sk):
        if sk.startswith("D:"):
            return self.dsem[sk[2:]]
        return self.sem[sk]

    def barrier(self, engs=("pe", "act", "dve", "pool", "sp"), with_dma=True):
        allw = [(e, self.cnt[e]) for e in self.ENG if self.cnt[e] > 0]
        if with_dma:
            allw += [("D:" + k, v) for k, v in self.dcnt.items()]
        else:
            allw = [(e, v) for e, v in allw if e in engs]
        for eng in engs:
            kn = self.known[eng]
            waits = []
            for sk, val in allw:
                if sk == eng and eng in ("pe", "sp"):
                    continue
                if kn.get(sk, 0) < val:
                    waits.append((sk, val))
                    kn[sk] = val
            if waits:
                self.streams[eng].append((waits, None, None))

    def replay(self):
        nc = self.nc
        with nc.Block() as block:
            def mk(ename):
                def body(e):
                    for waits, fn, inc in self.streams[ename]:
                        for sk, val in waits:
                            e.wait_ge(self.semh(sk), val)
                        if fn is not None:
                            ins = fn(e)
                            ins.then_inc(self.semh(inc[0]), inc[1])
                return body
            block.tensor(mk("pe"))
            block.scalar(mk("act"))
            block.vector(mk("dve"))
            block.gpsimd(mk("pool"))
            block.sync(mk("sp"))


class Arena:
    def __init__(self, ap, nwords):
        self.ap = ap
        self.n = nwords
        self.off = 0
        self.base = 0
        self.peak = 0

    def f32(self, n):
        a = self.off
        self.off += n
        self.peak = max(self.peak, self.off)
        assert self.off <= self.n, ("arena overflow", self.off, self.n)
        return self.ap[:, a:a + n]

    def bf(self, n):
        w = (n + 1) // 2
        return self.f32(w).bitcast(BF16)[:, 0:n]

    def mark(self):
        self.base = self.off

    def reset(self):
        self.off = self.base


def _pp_layout():
    lay = {}
    off = 0
    for name, n in [("ident", 128), ("tle", 128), ("tgt", 128), ("tge", 128), ("tlt", 128), ("ones", 128),
                    ("pcol", 1), ("rowv", 8), ("condT", 16), ("adab0T", 16), ("adab1T", 16),
                    ("aconvw", 64), ("aconvb", 16), ("bconvw", 248), ("bconvb", 8), ("blng", 8), ("blnb", 8),
                    ("clng", 16), ("clnb", 16)]:
        lay[name] = (off, n)
        off += n
    return lay, off


PPL, NPP = _pp_layout()


def _rows_layout():
    lay = {}
    off = 0
    for name, n in [("qidx", 256), ("adab0g", 1024), ("adab1g", 1024), ("lng0", 1024), ("lnb0", 1024),
                    ("lng1", 1024), ("lnb1", 1024), ("anormw", 1024), ("dtbias", 32), ("alog", 32), ("ad", 16),
                    ("cbs", 2048)]:
        lay[name] = (off, n)
        off += n
    return lay, off


RWL, NRW = _rows_layout()
NW = 53100


def build_nc():
    nc = bass.Bass("TRN2", target_bir_lowering=False)
    din = lambda name, shape: nc.dram_tensor(name, shape, F32, kind="ExternalInput").ap()
    xin = din("xin", [NTOK, D])
    pp_d = din("pp", [128, NPP])
    rows_d = din("rows", [1, NRW])
    h0f_d = din("h0f", [1024, 128])
    h0b_d = din("h0b", [1024, 128])
    wsT_d = din("wsT", [128, 2048])
    ada_d = [din("ada0", [1024, 3072]), din("ada1", [1024, 3072])]
    win0 = din("win0", [1024, 6176])
    wout0 = din("wout0", [2048, 1024])
    win1 = din("win1", [1024, 6144])
    wout1 = din("wout1", [2048, 1024])
    yout = nc.dram_tensor("yout", [NTOK, D], F32, kind="ExternalOutput").ap()
    hfo = nc.dram_tensor("hfo", [2, 1024, 128], F32, kind="ExternalOutput").ap()
    hbo = nc.dram_tensor("hbo", [2, 1024, 128], F32, kind="ExternalOutput").ap()

    with ExitStack() as es:
        kb = KB(nc, es)
        arena_t = es.enter_context(nc.sbuf_tensor("arena", [128, NW], F32))
        A = Arena(arena_t, NW)
        psh = [es.enter_context(nc.psum_tensor("ps%d" % i, [128, 512], F32)) for i in range(8)]
        ps = [h[:, :] for h in psh]
        psb = [h.bitcast(BF16)[:, :] for h in psh]
        rot = [0]

        def pbank():
            b = rot[0]
            rot[0] = (rot[0] + 1) % 7
            return b

        def E(eng, f, r=(), w=()):
            return kb.emit(eng, f, reads=r, writes=w)

        resid = A.f32(NT * D).rearrange("p (t d) -> p t d", t=NT)
        hT = A.bf(8 * NTOK).rearrange("p (k t) -> p k t", k=8)
        wslot = [A.bf(8 * 512).rearrange("p (k n) -> p k n", k=8) for _ in range(2)]
        pp = A.f32(NPP)
        gate_b = A.f32(2 * D).rearrange("p (r d) -> p r d", r=2)
        cb16 = {n: A.bf(128) for n in ("ident", "tle", "tgt", "tge", "tlt", "ones")}
        modT = A.f32(32).rearrange("p (j r) -> p j r", r=2)
        opsc = A.f32(16).rearrange("p (j r) -> p j r", r=2)
        sc32 = A.f32(16)
        scb = A.bf(16)
        epst = A.f32(4)
        pp_eps_ln = epst[:, 0:1]
        pp_eps_rms = epst[:, 1:2]
        pp_one = epst[:, 2:3]
        E("dve", lambda e: e.memset(epst[:, 0:1], LN_EPS), [], ["eps"])
        E("dve", lambda e: e.memset(epst[:, 1:2], RMS_EPS), [], ["eps"])
        E("dve", lambda e: e.memset(epst[:, 2:3], 1.0), [], ["eps"])
        A.mark()

        def P(name, a=0, b=None):
            o, n = PPL[name]
            if b is None:
                b = n
            return pp[:, o + a:o + b]

        def rowb(name, a=0, b=None):
            o, n = RWL[name]
            if b is None:
                b = n
            return rows_d[0:1, o + a:o + b].broadcast_to([128, b - a])

        slot_i = [0]

        def load_w(parts):
            s = slot_i[0]
            slot_i[0] = (s + 1) % 2
            names = []
            for i, (src, kt0, col0) in enumerate(parts):
                rows, n = src.shape
                nk = rows // 128
                rn = ("w", s, i)
                kb.dma("pool", "w%d_%d" % (s, i), wslot[s][:, kt0:kt0 + nk, col0:col0 + n],
                       src.rearrange("(k p) n -> p k n", p=128),
                       writes=[("w", s, j) for j in range(4)] if i == 0 else [rn])
                names.append(rn)
            return wslot[s], [("w", s, j) for j in range(4)]

        import os
        kstop = int(os.environ.get("KSTOP", "99"))
        stage = [0]

        class _Stop(Exception):
            pass

        def phase(pool=True):
            kb.barrier(("pe", "act", "dve", "pool", "sp") if pool else ("pe", "act", "dve", "sp"))
            A.reset()
            stage[0] += 1
            print("stage", stage[0])
            if stage[0] > kstop:
                raise _Stop()

        kb.dma("sp", "pp", pp, pp_d, writes=["pp"])
        for n in cb16:
            E("dve", lambda e, n=n: e.tensor_copy(cb16[n], P(n)), ["pp"], [("c16", n)])
        C16 = [("c16", n) for n in cb16]

        ada_bufs = {}

        def adaln(L, do_phase=True, part="ab"):
            deferred = []
            if do_phase:
                phase()
            if "a" in part:
                bcl = [[A.bf(128) for r in range(2)] for k in range(8)]
                adabg = A.f32(1024)
                ada_bufs[L] = (bcl, adabg)
                kb.dma("sp", "adabg", adabg, rowb("adab%dg" % L), writes=["adabg"])
                E("act", lambda e: e.activation(sc32, P("condT"), AF.Silu), ["pp"], ["sc32"])
                E("dve", lambda e: e.tensor_copy(scb, sc32), ["sc32"], ["scb"])
                for k in range(8):
                    for r in range(2):
                        E("dve", lambda e, k=k, r=r: e.tensor_scalar(bcl[k][r], cb16["ones"], sc32[:, 2 * k + r:2 * k + r + 1], None, ALU.mult),
                          ["sc32"] + C16, [("bcl", k, r)])
            bcl, adabg = ada_bufs[L]
            scbv = scb.rearrange("p (k r) -> p k r", r=2)
            for blk in (range(6) if part == "ab" else (range(4) if part == "a" else range(4, 6))):
                slot, wn = load_w([(ada_d[L][:, blk * 512:(blk + 1) * 512], 0, 0)])
                if blk < 4:
                    for j in range(4):
                        jj = blk * 4 + j

                        def f(e, slot=slot, j=j, jj=jj):
                            for k in range(8):
                                ins = e.matmul(ps[7][:, jj * 2:jj * 2 + 2], slot[:, k, j * 128:(j + 1) * 128], scbv[:, k, :],
                                               start=(k == 0), stop=(k == 7))
                            return ins
                        E("pe", f, wn + ["scb"], ["ps7"])
                else:
                    nh = blk - 4
                    for r in range(2):
                        b = pbank()

                        def f(e, slot=slot, r=r, b=b):
                            for k in range(8):
                                ins = e.matmul(ps[b], bcl[k][r], slot[:, k, :], start=(k == 0), stop=(k == 7))
                            return ins
                        E("pe", f, wn + [("bcl", k, r) for k in range(8)], [("ps", b)])
                        deferred.append(lambda r=r, b=b, nh=nh: E("dve", lambda e: e.tensor_tensor(gate_b[:, r, nh * 512:(nh + 1) * 512], ps[b], adabg[:, nh * 512:(nh + 1) * 512], ALU.add),
                                                                    [("ps", b), "adabg"], [("gate", r)]))
                if blk == 3:
                    def _mod():
                        E("dve", lambda e: e.tensor_tensor(modT, ps[7][:, 0:32].rearrange("p (j r) -> p j r", r=2),
                                                           P("adab%dT" % L).unsqueeze(2).broadcast_to([128, 16, 2]), ALU.add),
                          ["ps7", "pp"], ["modT"])
                        E("dve", lambda e: e.tensor_scalar(opsc, modT[:, 8:16, :], 1.0, None, ALU.add), ["modT"], ["opsc"])
                    deferred.append(_mod)
            return deferred

        def pos_embed():
            freqB = A.f32(256)
            posC = A.f32(512)
            kb.dma("sp", "q", freqB, rowb("qidx"), writes=["freq"])
            E("act", lambda e: e.activation(freqB, freqB, AF.Exp, scale=-math.log(10000.0) / 256.0), ["freq"], ["freq"])

            def sincos(scal, tag):
                argt = A.f32(512)
                kint = A.f32(512).bitcast(I32)
                mred = A.f32(512)
                dst = posC if tag == "c" else A.f32(512)
                E("dve", lambda e: e.tensor_scalar(argt[:, 0:256], freqB, scal, None, ALU.mult), ["freq", "pp"], [("argt", tag)])
                E("dve", lambda e: e.tensor_scalar(argt[:, 256:512], argt[:, 0:256], math.pi / 2, None, ALU.add), [("argt", tag)], [("argt", tag)])
                E("dve", lambda e: e.tensor_scalar(kint, argt, 1.0 / TWO_PI, None, ALU.mult), [("argt", tag)], [("kint", tag)])
                E("dve", lambda e: e.scalar_tensor_tensor(mred, kint, -TWO_PI, argt, ALU.mult, ALU.add), [("kint", tag), ("argt", tag)], [("mred", tag)])
                E("act", lambda e: e.activation(dst, mred, AF.Sin), [("mred", tag)], [("pos", tag)])
                return dst

            sincos(P("pcol"), "c")
            posRs = [sincos(P("rowv", s, s + 1), s) for s in range(8)]
            return posRs, posC

        def pos_add(posRs, posC):
            for s in range(8):
                t = 4 + s
                E("dve", lambda e, t=t, s=s: e.tensor_tensor(resid[:, t, 0:512], resid[:, t, 0:512], posRs[s], ALU.add),
                  [("res", t), ("pos", s)], [("res", t)])
                E("dve", lambda e, t=t: e.tensor_tensor(resid[:, t, 512:1024], resid[:, t, 512:1024], posC, ALU.add),
                  [("res", t), ("pos", "c")], [("res", t)])


        def transposes(tiles=range(NT)):
            for t in tiles:
                r = cond_of(t)
                for half in range(2):
                    b = pbank()

                    def f(e, t=t, half=half, b=b):
                        for j in range(4):
                            k = half * 4 + j
                            ins = e.transpose(ps[b][:, j * 128:(j + 1) * 128], resid[:, t, k * 128:(k + 1) * 128], P("ident"))
                        return ins
                    E("pe", f, [("res", t), "pp"], [("ps", b)])
                    for j in range(4):
                        k = half * 4 + j
                        if half == 0:
                            E("dve", lambda e, t=t, k=k, j=j, b=b, r=r: e.tensor_scalar(
                                hT[:, k, t * 128:(t + 1) * 128], ps[b][:, j * 128:(j + 1) * 128],
                                opsc[:, k, r:r + 1], modT[:, k, r:r + 1], ALU.mult, ALU.add),
                              [("ps", b), "opsc", "modT"], [("hT", t)])
                        else:
                            E("act", lambda e, t=t, k=k, j=j, b=b, r=r: e.activation(
                                hT[:, k, t * 128:(t + 1) * 128], ps[b][:, j * 128:(j + 1) * 128], AF.Identity,
                                bias=modT[:, k, r:r + 1], scale=opsc[:, k, r:r + 1]),
                              [("ps", b), "opsc", "modT"], [("hT", t)])

        HT_ALL = [("hT", t) for t in range(NT)]

        def hT_of_tb(tb):
            return [("hT", t) for t in range(tb * 4, tb * 4 + 4)]

        def outproj_partial(lhs_fn, lhs_res_fn, nkt, w_d, row0, first, tmp):
            for nh in range(2):
                parts = []
                done = 0
                slots = []
                while done < nkt:
                    n = min(8, nkt - done)
                    slot, wn = load_w([(w_d[row0 + done * 128:row0 + (done + n) * 128, nh * 512:(nh + 1) * 512], 0, 0)])
                    slots.append((slot, wn, done, n))
                    done += n
                pend_acc = []
                for c in range(NT):
                    r = cond_of(c)
                    b = pbank()

                    def f(e, c=c, b=b, slots=slots):
                        i = 0
                        for slot, wn, k0, n in slots:
                            for kk in range(n):
                                ins = e.matmul(ps[b], lhs_fn(k0 + kk, c), slot[:, kk, :], start=(i == 0), stop=(i == nkt - 1))
                                i += 1
                        return ins
                    rd = []
                    for slot, wn, k0, n in slots:
                        rd += wn
                    E("pe", f, rd + lhs_res_fn(c), [("ps", b)])
                    tv, tn = tmp[c % len(tmp)]
                    E("dve", lambda e, b=b, r=r, nh=nh, tv=tv: e.tensor_tensor(tv, ps[b], gate_b[:, r, nh * 512:(nh + 1) * 512], ALU.mult),
                      [("ps", b), ("gate", r)], [tn])
                    rs = resid[:, c, nh * 512:(nh + 1) * 512]

                    def _acc(rs=rs, tv=tv, tn=tn, c=c):
                        if first:
                            E("dve", lambda e: e.scalar_tensor_tensor(rs, rs, ALPHA, tv, ALU.mult, ALU.add), [tn, ("res", c)], [("res", c)])
                        else:
                            E("dve", lambda e: e.tensor_tensor(rs, rs, tv, ALU.add), [tn, ("res", c)], [("res", c)])
                    if pend_acc:
                        pend_acc.pop()()
                    pend_acc.append(_acc)
                if pend_acc:
                    pend_acc.pop()()

        def deepnorm_ln(L, do_phase=True):
            if do_phase:
                phase()
            lng = A.f32(1024)
            lnb = A.f32(1024)
            tmp = [A.f32(1024), A.f32(1024)]
            s1 = A.f32(12)
            s2 = A.f32(12)
            mean = A.f32(12)
            msq = A.f32(12)
            rstd = A.f32(12)
            nmr = A.f32(12)
            kb.dma("sp", "lng", lng, rowb("lng%d" % L), writes=["lng"])
            kb.dma("sp", "lnb", lnb, rowb("lnb%d" % L), writes=["lnb"])
            E("dve", lambda e: e.memset(s1, 0.0), [], ["s1"])
            E("dve", lambda e: e.memset(s2, 0.0), [], ["s2"])
            for c in range(NT):
                q = c % 2
                E("act", lambda e, c=c, q=q: e.activation(tmp[q], resid[:, c, :], AF.Copy, accum_out=s1[:, c:c + 1]), [("res", c), "s1"], [("lntmp", q), ("s1", c)])
                E("act", lambda e, c=c, q=q: e.activation(tmp[q], resid[:, c, :], AF.Square, accum_out=s2[:, c:c + 1]), [("res", c), "s2"], [("lntmp", q), ("s2", c)])
            SA = [("s1", c) for c in range(NT)] + [("s2", c) for c in range(NT)] + ["s1", "s2"]
            E("dve", lambda e: e.tensor_scalar(mean, s1, 1.0 / D, None, ALU.mult), SA, ["mean"])
            E("dve", lambda e: e.tensor_tensor(msq, mean, mean, ALU.mult), ["mean"], ["msq"])
            E("dve", lambda e: e.scalar_tensor_tensor(rstd, s2, 1.0 / D, msq, ALU.mult, ALU.subtract), SA + ["msq"], ["rstd"])
            E("act", lambda e: e.activation(rstd, rstd, AF.Ln, bias=pp_eps_ln), ["rstd", "eps"], ["rstd"])
            E("act", lambda e: e.activation(rstd, rstd, AF.Exp, scale=-0.5), ["rstd"], ["rstd"])
            E("dve", lambda e: e.scalar_tensor_tensor(nmr, mean, -1.0, rstd, ALU.mult, ALU.mult), ["mean", "rstd"], ["nmr"])
            for c in range(NT):
                q = c % 2
                E("act", lambda e, c=c, q=q: e.activation(tmp[q], resid[:, c, :], AF.Identity, bias=nmr[:, c:c + 1], scale=rstd[:, c:c + 1]),
                  [("res", c), "rstd", "nmr"], [("lntmp", q)])
                E("dve", lambda e, c=c, q=q: e.tensor_tensor(tmp[q], tmp[q], lng, ALU.mult), [("lntmp", q), "lng"], [("lntmp", q)])
                E("dve", lambda e, c=c, q=q: e.tensor_tensor(resid[:, c, :], tmp[q], lnb, ALU.add), [("lntmp", q), "lnb"], [("res", c)])

        def _body():
            pos_tabs = pos_embed()
            dfr0 = adaln(0, do_phase=False, part="a")
            for t in range(NT):
                kb.dma("pool", "x%d" % t, resid[:, t, :], xin[t * 128:(t + 1) * 128, :], writes=[("res", t)])
            for d_ in dfr0:
                d_()
            transposes(range(0, 4))
            pos_add(*pos_tabs)
            transposes(range(4, NT))
            for d_ in adaln(0, do_phase=False, part="b"):
                d_()

            phase(pool=False)
            sgT = A.bf(8 * NTOK).rearrange("p (k t) -> p k t", k=8)
            convout = A.f32(8 * NTOK).rearrange("p (k t) -> p k t", k=8)
            NPE = 19
            upad2 = [A.bf(UL + 2), A.bf(UL + 2)]
            accB = A.f32(UL + 2)
            diag = A.bf(NPE * 128).rearrange("p (k c) -> p k c", k=NPE)
            bwh = A.f32(248)
            tsA = A.f32(512)
            tsB = A.f32(512)
            tsC = A.f32(512)
            tsD = A.f32(512)
            tsQ = [A.f32(512), A.f32(512)]
            tsSet = [(tsB, tsC, tsD), (tsB, tsC, tsD)]
            for q_ in range(2):
                E("dve", lambda e, q_=q_: e.memset(upad2[q_], 0.0), [], [("upad", q_)])
            E("dve", lambda e: e.tensor_scalar(bwh, P("bconvw"), 0.5, None, ALU.mult), ["pp"], ["bwh"])
            VAL0, GLU0, GAT0 = 3104, 4128, 5152
            for ct in range(8):
                upad = upad2[ct % 2]
                upn = ("upad", ct % 2)
                slot, wn = load_w([(win0[:, VAL0 + ct * 128:VAL0 + (ct + 1) * 128], 0, 0),
                                   (win0[:, GLU0 + ct * 128:GLU0 + (ct + 1) * 128], 0, 128),
                                   (win0[:, GAT0 + ct * 128:GAT0 + (ct + 1) * 128], 0, 256)])
                E("dve", lambda e, ct=ct: e.tensor_tensor(diag, cb16["ident"].unsqueeze(1).broadcast_to([128, NPE, 128]),
                                                          bwh[:, ct * 31:ct * 31 + NPE].unsqueeze(2).broadcast_to([128, NPE, 128]), ALU.mult),
                  C16 + ["bwh"], ["diag"])
                for tb in range(3):
                    bs_ = []
                    for j in range(3):
                        b = pbank()
                        bs_.append(b)

                        def f(e, slot=slot, j=j, tb=tb, b=b):
                            for k in range(8):
                                ins = e.matmul(ps[b], slot[:, k, j * 128:(j + 1) * 128], hT[:, k, tb * 512:(tb + 1) * 512],
                                               start=(k == 0), stop=(k == 7))
                            return ins
                        E("pe", f, wn + hT_of_tb(tb), [("ps", b)])
                    bv, bg, bt = bs_
                    E("act", lambda e, bg=bg: e.activation(tsA, ps[bg], AF.Tanh, scale=0.5), [("ps", bg)], ["tsA"])
                    if tb == 0:
                        segs = [(UOFF[0], 0, 256), (UOFF[1], 256, 256)]
                    else:
                        segs = [(UOFF[2] + (tb - 1) * 512, 0, 512)]
                    for (uo, so, n) in segs:
                        E("dve", lambda e, uo=uo, so=so, n=n, bv=bv: e.scalar_tensor_tensor(
                            upad[:, uo:uo + n], tsA[:, so:so + n], 1.0, ps[bv][:, so:so + n], ALU.add, ALU.mult),
                          ["tsA", ("ps", bv)], [upn])
                    E("act", lambda e, bt=bt, ct=ct, tb=tb: e.activation(sgT[:, ct, tb * 512:(tb + 1) * 512], ps[bt], AF.Silu),
                      [("ps", bt)], [("sgT", ct, tb)])
                LCB = UL - 30
                HB = LCB // 2
                for k in range(NPE, 31):
                    wk = bwh[:, ct * 31 + k:ct * 31 + k + 1]
                    for hname, a0, a1 in (("accB0", 0, HB), ("accB1", HB, LCB)):
                        if k == NPE:
                            E("dve", lambda e, wk=wk, k=k, a0=a0, a1=a1: e.tensor_scalar(accB[:, a0:a1], upad[:, k + a0:k + a1], wk, None, ALU.mult),
                              [upn, "bwh"], [hname])
                        else:
                            E("dve", lambda e, wk=wk, k=k, a0=a0, a1=a1: e.scalar_tensor_tensor(accB[:, a0:a1], upad[:, k + a0:k + a1], wk, accB[:, a0:a1],
                                                                                               ALU.mult, ALU.add),
                              [upn, "bwh", hname], [hname])
                for (base, tok0, n) in [(0, 0, 256), (286, 256, 256), (572, 512, 512), (572 + 512, 1024, 512)]:
                    b = pbank()

                    def f(e, base=base, n=n, b=b):
                        for k in range(NPE):
                            ins = e.matmul(ps[b][:, 0:n], diag[:, k, :], upad[:, base + k:base + k + n], start=(k == 0), stop=(k == NPE - 1))
                        return ins
                    E("pe", f, ["diag", upn], [("ps", b)])
                    E("dve", lambda e, b=b, n=n, tok0=tok0, ct=ct, base=base: e.scalar_tensor_tensor(
                        convout[:, ct, tok0:tok0 + n], ps[b][:, 0:n], P("bconvb", ct, ct + 1), accB[:, base:base + n], ALU.add, ALU.add),
                      [("ps", b), "pp", "accB0", "accB1"], [("conv", ct)])
            for tb in range(3):
                tsl = slice(tb * 512, (tb + 1) * 512)
                mB, mC, mD = tsSet[tb % 2]
                nB, nC, nD = "tsB", "tsC", "tsD"
                b1 = pbank()
                b2 = pbank()

                def f(e, tsl=tsl, b1=b1):
                    for ct in range(8):
                        ins = e.matmul(ps[b1], P("ones"), convout[:, ct, tsl], start=(ct == 0), stop=(ct == 7))
                    return ins
                E("pe", f, [("conv", ct) for ct in range(8)] + ["pp"], [("ps", b1)])
                for ct in range(8):
                    q = ct % 2
                    E("act", lambda e, ct=ct, tsl=tsl, q=q: e.activation(tsQ[q], convout[:, ct, tsl], AF.Square), [("conv", ct)], [("tsQ", q)])
                    E("pe", lambda e, ct=ct, b2=b2, q=q: e.matmul(ps[b2], P("ones"), tsQ[q], start=(ct == 0), stop=(ct == 7)), [("tsQ", q), "pp"], [("ps", b2)])
                E("dve", lambda e, b1=b1: e.tensor_scalar(mB, ps[b1], 1.0 / 1024, None, ALU.mult), [("ps", b1)], [nB])
                E("dve", lambda e: e.tensor_tensor(mD, mB, mB, ALU.mult), [nB], [nD])
                E("dve", lambda e, b2=b2: e.scalar_tensor_tensor(mC, ps[b2], 1.0 / 1024, mD, ALU.mult, ALU.subtract), [("ps", b2), nD], [nC])
                E("act", lambda e: e.activation(mC, mC, AF.Ln, bias=pp_eps_ln), [nC, "eps"], [nC])
                E("act", lambda e: e.activation(mC, mC, AF.Exp, scale=-0.5), [nC], [nC])
                E("dve", lambda e: e.scalar_tensor_tensor(mD, mB, -1.0, mC, ALU.mult, ALU.mult), [nB, nC], [nD])
                for ct in range(8):
                    q = ct % 2
                    tq, tqn = (tsA, "tsA") if q == 0 else (tsQ[0], ("tsQ", 0))
                    E("dve", lambda e, ct=ct, tsl=tsl, tq=tq: e.tensor_tensor(tq, convout[:, ct, tsl], mC, ALU.mult), [("conv", ct), nC], [tqn])
                    E("dve", lambda e, tq=tq: e.tensor_tensor(tq, tq, mD, ALU.add), [tqn, nD], [tqn])
                    E("act", lambda e, ct=ct, tq=tq: e.activation(tq, tq, AF.Silu, bias=P("blnb", ct, ct + 1), scale=P("blng", ct, ct + 1)), [tqn, "pp"], [tqn])
                    E("dve", lambda e, ct=ct, tsl=tsl, tq=tq: e.tensor_tensor(sgT[:, ct, tsl], tq, sgT[:, ct, tsl], ALU.mult),
                      [tqn, ("sgT", ct, tb)], [("sgT", ct, tb)])
            outproj_partial(lambda kt, c: sgT[:, kt, c * 128:(c + 1) * 128],
                            lambda c: [("sgT", ct, c // 4) for ct in range(8)], 8, wout0, 1024, True, [(tsB, "tsB"), (tsC, "tsC"), (tsD, "tsD")])

            phase(pool=False)
            x_tok = A.bf(NT * 256).rearrange("p (c d) -> p c d", c=NT)
            xD = A.bf(NT * 256).rearrange("p (c d) -> p c d", c=NT)
            B_tok = A.bf(NT * 128).rearrange("p (c d) -> p c d", c=NT)
            BT = A.bf(NTOK)
            CT = A.bf(NTOK)
            zs = A.f32(NT * 256).rearrange("p (c d) -> p c d", c=NT)
            gin = A.bf(NT * 256).rearrange("p (c d) -> p c d", c=NT)
            yaT = A.bf(4 * NTOK).rearrange("p (k t) -> p k t", k=4)
            dtall = A.f32(NT * 32).rearrange("p (c h) -> p c h", c=NT)
            dtA = A.f32(NT * 32).rearrange("p (c h) -> p c h", c=NT)
            Etab = A.f32(NT * 160).rearrange("p (c m h) -> p c m h", c=NT, m=5)
            wdd = A.f32(NT * 32).rearrange("p (c h) -> p c h", c=NT)
            dtbias_b = A.f32(32)
            aneg_b = A.f32(32)
            ad_b = A.f32(16)
            normw_g = A.f32(256)
            f_state = A.f32(256)
            g_state = A.f32(256)
            tF = A.f32(256)
            ssrT = A.f32(2 * NT)
            h0ld = A.f32(256).rearrange("p (i n) -> p i n", i=2)
            sto = h0ld
            scratch0 = A.off
            rawpad = [A.bf(RL + 3), A.bf(RL + 3)]
            diagA = A.bf(16 * 128).rearrange("p (k c) -> p k c", k=16)
            xaT = [A.bf(NTOK), A.bf(NTOK)]
            A.off = scratch0
            V2 = lambda mk: [mk(), mk()]
            NEGM = {"f": A.bf(128), "b": A.bf(128)}
            Xf = V2(lambda: A.f32(512))
            Xb = V2(lambda: A.f32(512))
            Lf = V2(lambda: A.f32(512))
            Lb = V2(lambda: A.f32(512))
            MTf = V2(lambda: A.bf(512).rearrange("p (r i) -> p r i", r=4))
            MTb = V2(lambda: A.bf(512).rearrange("p (r i) -> p r i", r=4))
            xdtf = V2(lambda: A.bf(256))
            xdtb = V2(lambda: A.bf(256))
            xdd = V2(lambda: A.bf(256))
            Ssb = [A.f32(256), A.f32(256), A.f32(256)]
            fin = V2(lambda: A.bf(256))
            t1 = V2(lambda: A.f32(256))
            t2 = V2(lambda: A.f32(256))
            yv = V2(lambda: A.f32(256))
            ynb = V2(lambda: A.bf(256))
            ssr = V2(lambda: A.f32(2))
            optmp = [(Xf[0], ("X", "f", 0)), (Xf[1], ("X", "f", 1)), (Xb[0], ("X", "b", 0))]

            kb.dma("sp", "r1", dtbias_b, rowb("dtbias"), writes=["dtbias"])
            kb.dma("sp", "r2", aneg_b, rowb("alog"), writes=["aneg"])
            kb.dma("sp", "r3", ad_b, rowb("ad"), writes=["ad"])
            E("act", lambda e: e.activation(aneg_b, aneg_b, AF.Exp), ["aneg"], ["aneg"])
            E("dve", lambda e: e.tensor_scalar(aneg_b, aneg_b, -1.0, None, ALU.mult), ["aneg"], ["aneg"])
            masks32 = ["tle", "tgt", "ones", "tge", "tlt"]

            def bc_p(ap4):
                return ap4.unsqueeze(2).broadcast_to([128, 4, 64])

            def v464(ap):
                return ap.rearrange("p (r q) -> p r q", r=4)

            for g in range(4):
                kb.barrier(("pe", "act", "dve"), with_dma=False)
                for q_ in range(2):
                    E("dve", lambda e, q_=q_: e.memset(rawpad[q_], 0.0), [], [("rawpad", q_)])
                kb.dma("sp", "r4", normw_g, rowb("anormw", g * 256, (g + 1) * 256), writes=["normw"])
                S1, wn1 = load_w([(win0[:, 1024 + g * 256:1024 + g * 256 + 128], 0, 0),
                                  (win0[:, 1024 + g * 256 + 128:1024 + (g + 1) * 256], 0, 128),
                                  (win0[:, 2048 + g * 128:2048 + (g + 1) * 128], 0, 256),
                                  (win0[:, 2560 + g * 128:2560 + (g + 1) * 128], 0, 384)])
                S2, wn2 = load_w([(win0[:, g * 256:g * 256 + 128], 0, 0), (win0[:, g * 256 + 128:(g + 1) * 256], 0, 128),
                                  (win0[:, 3072:3104], 0, 256)])
                if g == 0:
                    for c in range(NT):
                        def f(e, c=c):
                            for k in range(8):
                                ins = e.matmul(ps[7][:, c * 32:(c + 1) * 32], hT[:, k, c * 128:(c + 1) * 128], S2[:, k, 256:288],
                                               start=(k == 0), stop=(k == 7))
                            return ins
                        E("pe", f, wn2 + [("hT", c)], ["ps7"])
                    E("dve", lambda e: e.tensor_tensor(dtall, ps[7][:, 0:384].rearrange("p (c h) -> p c h", c=NT),
                                                       dtbias_b.unsqueeze(1).broadcast_to([128, NT, 32]), ALU.add), ["ps7", "dtbias"], ["dtall"])
                    E("act", lambda e: e.activation(dtall, dtall, AF.Exp), ["dtall"], ["dtall"])
                    E("act", lambda e: e.activation(dtall, dtall, AF.Ln, bias=pp_one), ["dtall", "eps"], ["dtall"])
                    E("dve", lambda e: e.tensor_tensor(dtA, dtall, aneg_b.unsqueeze(1).broadcast_to([128, NT, 32]), ALU.mult), ["dtall", "aneg"], ["dtA"])
                    for c in range(NT):
                        b = pbank()

                        def f(e, c=c, b=b):
                            for m, mn in enumerate(masks32):
                                ins = e.matmul(ps[b][:, m * 32:(m + 1) * 32], P(mn), dtA[:, c, :], start=True, stop=True)
                            return ins
                        E("pe", f, ["dtA", "pp"], [("ps", b)])
                        E("act", lambda e, c=c, b=b: e.activation(Etab[:, c, :, :], ps[b][:, 0:160].rearrange("p (m h) -> p m h", m=5), AF.Exp),
                          [("ps", b)], [("Etab", c)])
                    ETALL = [("Etab", c) for c in range(NT)]
                    E("dve", lambda e: e.tensor_tensor(wdd[:, :, 0:16], dtall[:, :, 0:16], Etab[:, :, 1, 0:16], ALU.mult), ["dtall"] + ETALL, ["wdd"])
                    E("dve", lambda e: e.tensor_tensor(wdd[:, :, 16:32], dtall[:, :, 16:32], Etab[:, :, 4, 16:32], ALU.mult), ["dtall"] + ETALL, ["wdd"])
                def a1_vars(jt):
                    wt = (g * 2 + jt) if jt < 2 else (8 + g if jt == 2 else 12 + g)
                    q = jt % 2
                    return wt, rawpad[q], None, xaT[q], ("rawpad", q), ("acc", q), ("xaT", q)

                for jt_ in range(4):
                    wt_ = a1_vars(jt_)[0]
                    E("dve", lambda e, jt_=jt_, wt_=wt_: e.tensor_tensor(diagA[:, jt_ * 4:(jt_ + 1) * 4, :], cb16["ident"].unsqueeze(1).broadcast_to([128, 4, 128]),
                                                                  P("aconvw", wt_ * 4, wt_ * 4 + 4).unsqueeze(2).broadcast_to([128, 4, 128]), ALU.mult),
                      C16 + ["pp"], [("diagA", jt_)])

                def a1_part1(jt):
                    wt, RP, AC, XA, rpn, acn, xan = a1_vars(jt)
                    for tb in range(3):
                        b = pbank()

                        def f(e, tb=tb, b=b):
                            for k in range(8):
                                ins = e.matmul(ps[b], S1[:, k, jt * 128:(jt + 1) * 128], hT[:, k, tb * 512:(tb + 1) * 512],
                                               start=(k == 0), stop=(k == 7))
                            return ins
                        E("pe", f, wn1 + hT_of_tb(tb), [("ps", b)])
                        if tb == 0:
                            E("act", lambda e, b=b: e.activation(RP[:, ROFF[0]:ROFF[0] + 256], ps[b][:, 0:256], AF.Copy), [("ps", b)], [rpn])
                            E("act", lambda e, b=b: e.activation(RP[:, ROFF[1]:ROFF[1] + 256], ps[b][:, 256:512], AF.Copy), [("ps", b)], [rpn])
                        else:
                            o = ROFF[2] + (tb - 1) * 512
                            E("act", lambda e, b=b, o=o: e.activation(RP[:, o:o + 512], ps[b], AF.Copy), [("ps", b)], [rpn])

                def a1_part2(jt):
                    wt, RP, AC, XA, rpn, acn, xan = a1_vars(jt)
                    dst = {0: XA, 1: XA, 2: BT, 3: CT}[jt]
                    dname = {0: xan, 1: xan, 2: "BT", 3: "CT"}[jt]
                    for (base, tok0, n) in [(ROFF[0] - 2, 0, 256), (ROFF[1] - 2, 256, 256), (ROFF[2] - 2, 512, 512), (ROFF[2] - 2 + 512, 1024, 512)]:
                        b = pbank()

                        def f(e, base=base, n=n, b=b):
                            for k in range(4):
                                ins = e.matmul(ps[b][:, 0:n], diagA[:, jt * 4 + k, :], RP[:, base + k:base + k + n], start=(k == 0), stop=(k == 3))
                            return ins
                        E("pe", f, [("diagA", jt), rpn], [("ps", b)])
                        E("act", lambda e, b=b, n=n, tok0=tok0: e.activation(dst[:, tok0:tok0 + n], ps[b][:, 0:n], AF.Silu, bias=P("aconvb", wt, wt + 1)),
                          [("ps", b), "pp"], [dname])
                    if jt < 3:
                        src = XA if jt < 2 else BT
                        for cq in range(3):
                            b = pbank()

                            def f(e, cq=cq, b=b):
                                for j in range(4):
                                    c = cq * 4 + j
                                    ins = e.transpose(psb[b][:, j * 128:(j + 1) * 128], src[:, c * 128:(c + 1) * 128], cb16["ident"])
                                return ins
                            E("pe", f, [dname] + C16, [("ps", b)])
                            pv = psb[b][:, 0:512].rearrange("p (j d) -> p j d", j=4)
                            if jt < 2:
                                E("dve", lambda e, cq=cq, pv=pv: e.tensor_copy(x_tok[:, cq * 4:cq * 4 + 4, jt * 128:(jt + 1) * 128], pv),
                                  [("ps", b)], ["x_tok"])
                            else:
                                E("dve", lambda e, cq=cq, pv=pv: e.tensor_copy(B_tok[:, cq * 4:cq * 4 + 4, :], pv), [("ps", b)], ["B_tok"])

                a1_part1(0)
                for jt in range(4):
                    if jt + 1 < 4:
                        a1_part1(jt + 1)
                    a1_part2(jt)
                hsl = slice(4 * g, 4 * g + 4)
                hslb = slice(16 + 4 * g, 16 + 4 * g + 4)
                E("dve", lambda e: e.tensor_tensor(xD.rearrange("p c (r q) -> p c r q", r=4), x_tok.rearrange("p c (r q) -> p c r q", r=4),
                                                   ad_b[:, hsl].unsqueeze(1).unsqueeze(3).broadcast_to([128, NT, 4, 64]), ALU.mult),
                  ["x_tok", "ad"], ["xD"])
                kb.barrier(("pe", "act", "dve"), with_dma=False)

                def z_pair(cp):
                    b = pbank()

                    def f(e, cp=cp, b=b):
                        for j in range(2):
                            c = cp * 2 + j
                            for k in range(8):
                                ins = e.matmul(ps[b][:, j * 256:(j + 1) * 256], hT[:, k, c * 128:(c + 1) * 128], S2[:, k, 0:256],
                                               start=(k == 0), stop=(k == 7))
                        return ins
                    E("pe", f, wn2 + [("hT", cp * 2), ("hT", cp * 2 + 1)], [("ps", b)])
                    E("act", lambda e, cp=cp, b=b: e.activation(zs[:, cp * 2:cp * 2 + 2, :], ps[b].rearrange("p (j d) -> p j d", j=2), AF.Silu),
                      [("ps", b)], ["zs"])

                def load_state(h0_d, st, stname):
                    kb.dma("sp", "h0", h0ld, h0_d[g * 256:(g + 1) * 256, :].rearrange("(i p) n -> p i n", p=128), writes=["h0ld"])
                    b = pbank()

                    def f(e, b=b):
                        for i in range(2):
                            ins = e.transpose(ps[b][:, i * 128:(i + 1) * 128], h0ld[:, i, :], P("ident"))
                        return ins
                    E("pe", f, ["h0ld", "pp"], [("ps", b)])
                    E("dve", lambda e, b=b: e.tensor_copy(st, ps[b][:, 0:256]), [("ps", b)], [stname])

                def store_state(st, stname, out_d, s):
                    b = pbank()

                    def f(e, b=b):
                        for i in range(2):
                            ins = e.transpose(ps[b][:, i * 128:(i + 1) * 128], st[:, i * 128:(i + 1) * 128], P("ident"))
                        return ins
                    E("pe", f, [stname, "pp"], [("ps", b)])
                    E("act", lambda e, b=b: e.activation(sto, ps[b][:, 0:256].rearrange("p (i n) -> p i n", i=2), AF.Copy), [("ps", b)], ["h0ld"])
                    kb.dma("sp", "so", out_d[s, g * 256:(g + 1) * 256, :].rearrange("(i p) n -> p i n", p=128), sto, reads=["h0ld"])

                order1 = []
                for s, (c0, ncs) in enumerate(SEQS):
                    for c in range(c0 + ncs - 1, c0 - 1, -1):
                        order1.append((s, c, c == c0 + ncs - 1, c == c0))

                def p1_pre(i):
                    s, c, first, last = order1[i]
                    if s == 2 and last:
                        return
                    v = i % 2
                    E("dve", lambda e: e.tensor_tensor(v464(xdd[v]), v464(x_tok[:, c, :]), bc_p(wdd[:, c, hslb]), ALU.mult),
                      ["x_tok", "wdd"], [("xdd", v)])
                    b = pbank()
                    E("pe", lambda e: e.matmul(ps[b][:, 0:256], B_tok[:, c, :], xdd[v], start=True, stop=True), ["B_tok", ("xdd", v)], [("ps", b)])
                    E("act", lambda e: e.activation(Ssb[v], ps[b][:, 0:256], AF.Copy), [("ps", b)], [("Ssb", v)])
                p1_pre(0)
                for i, (s, c, first, last) in enumerate(order1):
                    v = i % 2
                    if i % 2 == 0:
                        z_pair(i // 2)
                    has_chain = not (s == 2 and last)
                    if (not has_chain) and i + 1 < len(order1):
                        p1_pre(i + 1)
                    if first:
                        if s == 2:
                            load_state(h0b_d, g_state, "g_state")
                        else:
                            E("dve", lambda e: e.memset(g_state, 0.0), [], ["g_state"])
                    E("act", lambda e: e.activation(gin[:, c, :], g_state, AF.Copy), ["g_state"], [("gin", c)])
                    if not has_chain:
                        continue
                    E("dve", lambda e: e.tensor_tensor(v464(tF), v464(g_state), bc_p(Etab[:, c, 2, hslb]), ALU.mult), ["g_state"] + ETALL, ["tF"])
                    if i + 1 < len(order1):
                        p1_pre(i + 1)
                    E("dve", lambda e: e.tensor_tensor(g_state, tF, Ssb[v], ALU.add), ["tF", ("Ssb", v)], ["g_state"])
                    if last and s < 2:
                        store_state(g_state, "g_state", hbo, s)

                order2 = []
                for s, (c0, ncs) in enumerate(SEQS):
                    for c in range(c0, c0 + ncs):
                        order2.append((s, c, c == c0, c == c0 + ncs - 1))

                E("dve", lambda e: e.memset(ssrT, 0.0), [], [("ssrc", i_) for i_ in range(NT)])
                E("dve", lambda e: e.tensor_scalar(NEGM["f"], cb16["tgt"], -30000.0, None, ALU.mult), C16, [("NEGM", "f")])
                E("dve", lambda e: e.tensor_scalar(NEGM["b"], cb16["tlt"], -30000.0, None, ALU.mult), C16, [("NEGM", "b")])

                def p2_step(step):
                    ia, ib, ic = step, step - 1, step - 2
                    hasA, hasB, hasC = ia < n2, 0 <= ib < n2, 0 <= ic < n2
                    if hasA:
                        sA, cA, firstA, lastA = order2[ia]
                        vA = ia % 2
                        cslA = slice(cA * 128, (cA + 1) * 128)
                        xvA = v464(x_tok[:, cA, :])
                    if hasB:
                        sB, cB, firstB, lastB = order2[ib]
                        vB = ib % 2
                        cslB = slice(cB * 128, (cB + 1) * 128)
                    if hasC:
                        sC, cC, firstC, lastC = order2[ic]
                        vC = ic % 2
                        cslC = slice(cC * 128, (cC + 1) * 128)
                    if hasC:
                        ssc = ssrT[:, 2 * ic:2 * ic + 2]
                        sscn = ("ssrc", ic)
                        E("act", lambda e: e.activation(t1[vC], yv[vC], AF.Square, accum_out=ssc[:, 0:1]),
                          [("yv", vC), sscn, ("t1", vC)], [("t1", vC), sscn])
                        E("act", lambda e: e.activation(ssc[:, 1:2], ssc[:, 0:1], AF.Ln, bias=pp_eps_rms, scale=1.0 / 256), [sscn, "eps"], [sscn])
                        E("act", lambda e: e.activation(ssc[:, 1:2], ssc[:, 1:2], AF.Exp, scale=-0.5), [sscn], [sscn])
                    if hasB:
                        by, bof, bob = pbank(), pbank(), pbank()

                        def f(e):
                            e.matmul(ps[by][:, 0:256], cb16["ident"], xD[:, cB, :], start=True, stop=False)
                            for r in range(4):
                                e.matmul(ps[by][:, r * 64:(r + 1) * 64], MTf[vB][:, r, :], xdtf[vB][:, r * 64:(r + 1) * 64], start=False, stop=False)
                                ins = e.matmul(ps[by][:, r * 64:(r + 1) * 64], MTb[vB][:, r, :], xdtb[vB][:, r * 64:(r + 1) * 64], start=False, stop=(r == 3))
                            return ins
                        E("pe", f, C16 + ["xD", ("MT", "f", vB), ("MT", "b", vB), ("xdtf", vB), ("xdtb", vB)], [("ps", by)])
                        E("pe", lambda e: e.matmul(ps[bof][:, 0:256], CT[:, cslB], fin[vB], start=True, stop=True), ["CT", ("fin", vB)], [("ps", bof)])
                        E("pe", lambda e: e.matmul(ps[bob][:, 0:256], CT[:, cslB], gin[:, cB, :], start=True, stop=True), ["CT", ("gin", cB)], [("ps", bob)])
                    if hasA:
                        segb_ = {}
                        for dr, msk, hs_, X in (("f", "tle", hsl, Xf), ("b", "tge", hslb, Xb)):
                            E("dve", lambda e: e.tensor_tensor(X[vA].rearrange("p (r i) -> p r i", r=4), P(msk).unsqueeze(1).broadcast_to([128, 4, 128]),
                                                               dtA[:, cA, hs_].unsqueeze(2).broadcast_to([128, 4, 128]), ALU.mult),
                              ["pp", "dtA"], [("X", dr, vA)])
                        bcb = pbank()
                        E("pe", lambda e: e.matmul(ps[bcb][:, 0:128], BT[:, cslA], CT[:, cslA], start=True, stop=True), ["BT", "CT"], [("ps", bcb)])
                        for dr, lhs, X in (("f", "tgt", Xf), ("b", "tlt", Xb)):
                            b = pbank()
                            segb_[dr] = b
                            def f(e, dr=dr, lhs=lhs, X=X, b=b):
                                e.matmul(ps[b].rearrange("p (r i) -> p r i", r=4), cb16["ident"], NEGM[dr].unsqueeze(1).broadcast_to([128, 4, 128]),
                                         start=True, stop=False)
                                return e.matmul(ps[b], P(lhs), X[vA], start=False, stop=True)
                            E("pe", f, C16 + ["pp", ("X", dr, vA), ("NEGM", dr)], [("ps", b)])
                    if hasC:
                        E("dve", lambda e: e.scalar_tensor_tensor(ynb[vC], yv[vC], ssc[:, 1:2], normw_g, ALU.mult, ALU.mult),
                          [("yv", vC), sscn, "normw"], [("ynb", vC)])
                        bt = pbank()

                        def f(e):
                            for j in range(2):
                                ins = e.transpose(psb[bt][:, j * 128:(j + 1) * 128], ynb[vC][:, j * 128:(j + 1) * 128], cb16["ident"])
                            return ins
                        E("pe", f, [("ynb", vC)] + C16, [("ps", bt)])
                    if hasB:
                        E("dve", lambda e: e.tensor_tensor(v464(t1[vB]), v464(ps[bof][:, 0:256]), bc_p(Etab[:, cB, 0, hsl]), ALU.mult), [("ps", bof)] + ETALL, [("t1", vB)])
                        for r in range(4):
                            E("act", lambda e, r=r: e.activation(t2[vB][:, r * 64:(r + 1) * 64], ps[bob][:, r * 64:(r + 1) * 64], AF.Copy,
                                                                 scale=Etab[:, cB, 3, 16 + 4 * g + r:16 + 4 * g + r + 1]), [("ps", bob)] + ETALL, [("t2", vB)])
                    if hasA:
                        E("dve", lambda e: e.tensor_tensor(v464(xdtf[vA]), xvA, bc_p(dtall[:, cA, hsl]), ALU.mult), ["x_tok", "dtall"], [("xdtf", vA)])
                        E("dve", lambda e: e.tensor_tensor(v464(xdtb[vA]), xvA, bc_p(dtall[:, cA, hslb]), ALU.mult), ["x_tok", "dtall"], [("xdtb", vA)])
                        needS = not (sA == 2 and lastA)
                        if needS:
                            E("dve", lambda e: e.tensor_tensor(v464(xdd[vA]), xvA, bc_p(wdd[:, cA, hsl]), ALU.mult), ["x_tok", "wdd"], [("xdd", vA)])
                    yv_rest = []
                    if hasB:
                        E("dve", lambda e: e.tensor_tensor(yv[vB], ps[by][:, 0:256], t1[vB], ALU.add), [("ps", by), ("t1", vB)], [("yv", vB)])
                        yv_rest = [lambda: E("dve", lambda e: e.tensor_tensor(yv[vB], yv[vB], t2[vB], ALU.add), [("yv", vB), ("t2", vB)], [("yv", vB)]),
                                   lambda: E("dve", lambda e: e.tensor_tensor(yv[vB], yv[vB], zs[:, cB, :], ALU.mult), [("yv", vB), "zs"], [("yv", vB)])]
                    if not hasA:
                        for op_ in yv_rest:
                            op_()
                        yv_rest = []
                    if hasA:
                        if needS:
                            bS = pbank()
                            E("pe", lambda e: e.matmul(ps[bS][:, 0:256], B_tok[:, cA, :], xdd[vA], start=True, stop=True), ["B_tok", ("xdd", vA)], [("ps", bS)])
                        for dr, Lx in (("f", Lf), ("b", Lb)):
                            b = segb_[dr]
                            E("act", lambda e: e.activation(Lx[vA], ps[b], AF.Exp), [("ps", b)], [("L", dr, vA)])
                        if needS:
                            E("act", lambda e: e.activation(Ssb[ia % 3], ps[bS][:, 0:256], AF.Copy), [("ps", bS)], [("Ssb", ia % 3)])
                        if not firstA or ia > 0:
                            if ia > 0:
                                sP, cP, firstP, lastP = order2[ia - 1]
                                if not firstA or sP < 2:
                                    E("dve", lambda e: e.tensor_tensor(v464(tF), v464(f_state), bc_p(Etab[:, cP, 2, hsl]), ALU.mult), ["f_state"] + ETALL, ["tF"])
                                    if yv_rest:
                                        yv_rest.pop(0)()
                                    E("dve", lambda e: e.tensor_tensor(f_state, tF, Ssb[(ia - 1) % 3], ALU.add), ["tF", ("Ssb", (ia - 1) % 3)], ["f_state"])
                                    if yv_rest:
                                        yv_rest.pop(0)()
                                    if firstA:
                                        store_state(f_state, "f_state", hfo, sP)
                        for op_ in yv_rest:
                            op_()
                        yv_rest = []
                        if firstA:
                            if sA == 2:
                                load_state(h0f_d, f_state, "f_state")
                            else:
                                E("dve", lambda e: e.memset(f_state, 0.0), [], ["f_state"])
                        E("act", lambda e: e.activation(fin[vA], f_state, AF.Copy), ["f_state"], [("fin", vA)])
                        for dr, Lx, MTx in (("f", Lf, MTf), ("b", Lb, MTb)):
                            E("dve", lambda e: e.tensor_tensor(MTx[vA], Lx[vA].rearrange("p (r i) -> p r i", r=4),
                                                               ps[bcb][:, 0:128].unsqueeze(1).broadcast_to([128, 4, 128]), ALU.mult),
                              [("L", dr, vA), ("ps", bcb)], [("MT", dr, vA)])
                    if hasC:
                        E("act", lambda e: e.activation(yaT[:, (g % 2) * 2:(g % 2) * 2 + 2, cslC], psb[bt][:, 0:256].rearrange("p (j d) -> p j d", j=2), AF.Copy),
                          [("ps", bt)], [("yaT", cC)])

                n2 = len(order2)
                for step in range(n2 + 2):
                    p2_step(step)
                if g % 2 == 1:
                    outproj_partial(lambda kt, c: yaT[:, kt, c * 128:(c + 1) * 128], lambda c: [("yaT", c)], 4, wout0, (g - 1) * 256, False, optmp)

            phase()
            dfr = adaln(1, do_phase=False)
            deepnorm_ln(0, do_phase=False)
            for d_ in dfr:
                d_()
            transposes()
            phase()
            V = A.bf(NT * 2048).rearrange("p (c g d) -> p c g d", c=NT, g=16)
            Kt = A.f32(2048).rearrange("p (g i) -> p g i", g=16)
            bsB = A.f32(2048)
            wsTb = A.bf(2048).rearrange("p (g i) -> p g i", g=16)
            ugT = A.f32(NTOK)
            tsA = A.f32(512)
            tsB = A.f32(512)
            s1 = A.f32(48)
            s2 = A.f32(48)
            mean = A.f32(12)
            msq = A.f32(12)
            var = A.f32(12)
            nmr = A.f32(12)
            kb.dma("pool", "wsT", wsTb, wsT_d.rearrange("p (g i) -> p g i", g=16), writes=["wsTb"])
            kb.dma("sp", "bsB", bsB, rowb("cbs"), writes=["bsB"])
            for gq in range(4):
                b = pbank()

                def f(e, gq=gq, b=b):
                    for j in range(4):
                        ins = e.matmul(ps[b][:, j * 128:(j + 1) * 128], cb16["ones"], wsTb[:, gq * 4 + j, :], start=True, stop=True)
                    return ins
                E("pe", f, ["wsTb"] + C16, [("ps", b)])
                for j in range(4):
                    gg = gq * 4 + j
                    E("dve", lambda e, b=b, j=j, gg=gg: e.scalar_tensor_tensor(Kt[:, gg, :], ps[b][:, j * 128:(j + 1) * 128], P("clnb", gg, gg + 1),
                                                                              bsB[:, gg * 128:(gg + 1) * 128], ALU.mult, ALU.add),
                      [("ps", b), "pp", "bsB"], ["Kt"])
            E("dve", lambda e: e.memset(s2, 0.0), [], ["s2"])
            E("dve", lambda e: e.memset(s1, 0.0), [], ["s1"])
            for vb in range(4):
                slot, wn = load_w([(win1[:, 2048 + vb * 512:2048 + (vb + 1) * 512], 0, 0)])
                for c in range(NT):
                    b = pbank()

                    def f(e, slot=slot, c=c, b=b):
                        for k in range(8):
                            ins = e.matmul(ps[b], hT[:, k, c * 128:(c + 1) * 128], slot[:, k, :], start=(k == 0), stop=(k == 7))
                        return ins
                    E("pe", f, wn + [("hT", c)], [("ps", b)])
                    E("act", lambda e, c=c, vb=vb, b=b: e.activation(V[:, c, vb * 4:(vb + 1) * 4, :].rearrange("p g d -> p (g d)"), ps[b], AF.Copy,
                                                                     accum_out=s1[:, c * 4 + vb:c * 4 + vb + 1]),
                      [("ps", b), "s1"], [("V", c), ("s1", c, vb)])
                    E("act", lambda e, c=c, vb=vb, b=b: e.activation(tsA, ps[b], AF.Square, accum_out=s2[:, c * 4 + vb:c * 4 + vb + 1]),
                      [("ps", b), "s2"], ["tsAj", ("s2", c, vb)])
            SALL = [("s1", c, vb) for c in range(NT) for vb in range(4)] + [("s2", c, vb) for c in range(NT) for vb in range(4)]
            E("dve", lambda e: e.reduce_sum(mean, s1.rearrange("p (c v) -> p c v", v=4), axis=AX.X), SALL, ["mean"])
            E("dve", lambda e: e.reduce_sum(var, s2.rearrange("p (c v) -> p c v", v=4), axis=AX.X), SALL, ["var"])
            E("dve", lambda e: e.tensor_scalar(mean, mean, 1.0 / 2048, None, ALU.mult), ["mean"], ["mean"])
            E("dve", lambda e: e.tensor_tensor(msq, mean, mean, ALU.mult), ["mean"], ["msq"])
            E("dve", lambda e: e.scalar_tensor_tensor(var, var, 1.0 / 2048, msq, ALU.mult, ALU.subtract), ["var", "msq"], ["var"])
            E("act", lambda e: e.activation(var, var, AF.Ln, bias=pp_eps_ln), ["var", "eps"], ["var"])
            E("act", lambda e: e.activation(var, var, AF.Exp, scale=-0.5), ["var"], ["var"])
            E("dve", lambda e: e.scalar_tensor_tensor(nmr, mean, -1.0, var, ALU.mult, ALU.mult), ["mean", "var"], ["nmr"])
            for c in range(NT):
                vc = V[:, c, :, :].rearrange("p g d -> p (g d)")
                E("dve", lambda e, c=c, vc=vc: e.tensor_scalar(vc, vc, var[:, c:c + 1], nmr[:, c:c + 1], ALU.mult, ALU.add),
                  [("V", c), "var", "nmr"], [("V", c)])
            for g in range(16):
                slot, wn = load_w([(win1[:, g * 128:(g + 1) * 128], 0, 0), (win1[:, 4096 + g * 128:4096 + (g + 1) * 128], 0, 128)])
                for tb in range(3):
                    bu = pbank()
                    bg = pbank()
                    for j, b in ((0, bu), (1, bg)):
                        def f(e, slot=slot, j=j, tb=tb, b=b):
                            for k in range(8):
                                ins = e.matmul(ps[b], slot[:, k, j * 128:(j + 1) * 128], hT[:, k, tb * 512:(tb + 1) * 512], start=(k == 0), stop=(k == 7))
                            return ins
                        E("pe", f, wn + hT_of_tb(tb), [("ps", b)])
                    E("act", lambda e, bg=bg: e.activation(tsA, ps[bg], AF.Silu), [("ps", bg)], ["tsAj"])
                    E("dve", lambda e, bu=bu, tb=tb: e.tensor_tensor(ugT[:, tb * 512:(tb + 1) * 512], ps[bu], tsA, ALU.mult), [("ps", bu), "tsAj"], [("ug", tb)])
                for cq in range(3):
                    b = pbank()

                    def f(e, cq=cq, g=g, b=b):
                        for j in range(4):
                            ins = e.matmul(ps[b][:, j * 128:(j + 1) * 128], V[:, cq * 4 + j, g, :], wsTb[:, g, :], start=True, stop=True)
                        return ins
                    E("pe", f, [("V", cq * 4 + j) for j in range(4)] + ["wsTb"], [("ps", b)])
                    E("dve", lambda e, b=b, g=g: e.scalar_tensor_tensor(tsB.rearrange("p (j i) -> p j i", j=4), ps[b].rearrange("p (j i) -> p j i", j=4),
                                                                        P("clng", g, g + 1), Kt[:, g, :].unsqueeze(1).broadcast_to([128, 4, 128]),
                                                                        ALU.mult, ALU.add),
                      [("ps", b), "pp", "Kt"], ["tsB"])
                    E("dve", lambda e, cq=cq, g=g: e.tensor_tensor(V[:, cq * 4:cq * 4 + 4, g, :], tsB.rearrange("p (j i) -> p j i", j=4),
                                                                   ugT[:, cq * 512:(cq + 1) * 512].rearrange("p (j i) -> p j i", j=4), ALU.mult),
                      ["tsB", ("ug", cq)], [("V", cq * 4 + j) for j in range(4)])
            outproj_partial(lambda kt, c: V[:, c, kt, :], lambda c: [("V", c)], 16, wout1, 0, True, [(tsB, "tsB"), (tsA, "tsAj"), (ugT[:, 0:512], ("ug", 0))])
            deepnorm_ln(1)
            for t in range(NT):
                kb.dma("sp", "y%d" % t, yout[t * 128:(t + 1) * 128, :], resid[:, t, :], reads=[("res", t)])
        try:
            _body()
        except _Stop:
            kb.barrier()
            for t in range(NT):
                kb.dma("sp", "y%d" % t, yout[t * 128:(t + 1) * 128, :], resid[:, t, :], reads=[("res", t)])
        kb.barrier()
        print("arena peak words", A.peak, "instr counts", kb.cnt, "dma sems", len(kb.dsem))
        kb.replay()
    return nc


def _host_pp(core, inp):
    pp = np.zeros((128, NPP), np.float32)
    p = np.arange(128)

    def put(name, arr):
        o, n = PPL[name]
        pp[:, o:o + n] = np.asarray(arr, np.float32).reshape(128, n)
    put("ident", np.eye(128))
    t = p[:, None]
    i = p[None, :]
    put("tle", (t <= i))
    put("tgt", (t > i))
    put("tge", (t >= i))
    put("tlt", (t < i))
    put("ones", np.ones((128, 128)))
    put("pcol", (p % 64)[:, None])
    put("rowv", (2 * np.arange(8)[None, :] + (p // 64)[:, None]))
    cond = np.stack([inp["c_ctx"], inp["c"][core]], 0)
    put("condT", cond.reshape(2, 8, 128).transpose(2, 1, 0).reshape(128, 16))
    put("adab0T", inp["ada_b_l0"][:2048].reshape(16, 128).T)
    put("adab1T", inp["ada_b_l1"][:2048].reshape(16, 128).T)
    put("aconvw", inp["a_conv_w_l0"].reshape(4, 16, 128).transpose(2, 1, 0).reshape(128, 64))
    put("aconvb", inp["a_conv_b_l0"].reshape(16, 128).T)
    put("bconvw", inp["b_conv_w_l0"].reshape(31, 8, 128).transpose(2, 1, 0).reshape(128, 248))
    put("bconvb", inp["b_conv_b_l0"].reshape(8, 128).T)
    put("blng", inp["b_ln_g_l0"].reshape(8, 128).T)
    put("blnb", inp["b_ln_b_l0"].reshape(8, 128).T)
    put("clng", inp["c_ln_g_l1"].reshape(16, 128).T)
    put("clnb", inp["c_ln_b_l1"].reshape(16, 128).T)
    return pp


def _host_rows(inp):
    rows = np.zeros((1, NRW), np.float32)

    def put(name, arr):
        o, n = RWL[name]
        rows[0, o:o + n] = np.asarray(arr, np.float32).reshape(n)
    put("qidx", np.arange(256))
    put("adab0g", inp["ada_b_l0"][2048:])
    put("adab1g", inp["ada_b_l1"][2048:])
    put("lng0", inp["ln_g_l0"])
    put("lnb0", inp["ln_b_l0"])
    put("lng1", inp["ln_g_l1"])
    put("lnb1", inp["ln_b_l1"])
    put("anormw", inp["a_norm_w_l0"])
    put("dtbias", np.concatenate([inp["a_dt_bias_f_l0"], inp["a_dt_bias_b_l0"]]))
    put("alog", np.concatenate([inp["a_log_f_l0"], inp["a_log_b_l0"]]))
    put("ad", inp["a_d_l0"])
    put("cbs", inp["c_bs_l1"])
    return rows


_NC_CACHE = {}


def kernel(**inputs):
    inp = {k: np.asarray(v) for k, v in inputs.items()}
    if "nc" not in _NC_CACHE:
        _NC_CACHE["nc"] = build_nc()
    nc = _NC_CACHE["nc"]
    rows = _host_rows(inp)
    wsT = np.ascontiguousarray(inp["c_ws_l1"].transpose(2, 0, 1).reshape(128, 2048)).astype(np.float32)
    shared = {
        "rows": rows, "wsT": wsT,
        "ada0": np.ascontiguousarray(inp["ada_w_l0"], np.float32), "ada1": np.ascontiguousarray(inp["ada_w_l1"], np.float32),
        "win0": np.ascontiguousarray(inp["w_in_l0"], np.float32), "wout0": np.ascontiguousarray(inp["w_out_l0"], np.float32),
        "win1": np.ascontiguousarray(inp["w_in_l1"], np.float32), "wout1": np.ascontiguousarray(inp["w_out_l1"], np.float32),
    }
    in_maps = []
    for c in range(8):
        xin = np.concatenate([inp["x_prompt"][2 * c], inp["x_prompt"][2 * c + 1], inp["x_sample"][c]], 0).astype(np.float32)
        m = dict(shared)
        m["xin"] = np.ascontiguousarray(xin)
        m["pp"] = _host_pp(c, inp)
        m["h0f"] = np.ascontiguousarray(inp["state_ssd_fwd_l0"][c].reshape(1024, 128), np.float32)
        m["h0b"] = np.ascontiguousarray(inp["state_ssd_bwd_l0"][c].reshape(1024, 128), np.float32)
        in_maps.append(m)
    res = run_bass_kernel_spmd(nc, in_maps, core_ids=list(range(8)))
    y_p = np.zeros((16, 256, D), np.float32)
    y_s = np.zeros((8, 1024, D), np.float32)
    hf = np.zeros((16, 16, 64, 128), np.float32)
    hb = np.zeros((16, 16, 64, 128), np.float32)
    for c in range(8):
        r = res.results[c]
        y = r["yout"]
        y_p[2 * c] = y[0:256]
        y_p[2 * c + 1] = y[256:512]
        y_s[c] = y[512:1536]
        hf[2 * c:2 * c + 2] = r["hfo"].reshape(2, 16, 64, 128)
        hb[2 * c:2 * c + 2] = r["hbo"].reshape(2, 16, 64, 128)
    return (y_p, y_s, hf, hb)
```

```python
import math
import numpy as np
from contextlib import ExitStack
import concourse.bass as bass
import concourse.mybir as mybir
from concourse.bass_utils import run_bass_kernel_spmd

F32 = mybir.dt.float32
BF16 = mybir.dt.bfloat16
I32 = mybir.dt.int32
ALU = mybir.AluOpType
AF = mybir.ActivationFunctionType
AX = mybir.AxisListType

D = 1024
NT = 12
NTOK = 1536
ALPHA = 4.0 ** 0.25
LN_EPS = 1e-5
RMS_EPS = 1e-5
TWO_PI = 2.0 * math.pi
SEQS = [(0, 2), (2, 2), (4, 8)]
UOFF = [15, 301, 587]
UL = 1626
ROFF = [2, 261, 520]
RL = 1545


def cond_of(t):
    return 0 if t < 4 else 1


class _Rec:
    def __init__(self):
        self.calls = []

    def __getattr__(self, name):
        def m(*a, **k):
            self.calls.append((name, a, k))
            return None
        return m


class KB:
    ENG = ("pe", "act", "dve", "pool", "sp")

    def __init__(self, nc, es):
        self.nc = nc
        self.es = es
        self.sem = {e: es.enter_context(nc.semaphore("s_" + e)) for e in self.ENG}
        self.cnt = {e: 0 for e in self.ENG}
        self.known = {e: {} for e in self.ENG}
        self.streams = {e: [] for e in self.ENG}
        self.dsem = {}
        self.dcnt = {}
        self.resw = {}
        self.resr = {}

    def _deps(self, reads, writes):
        deps = []
        for r in reads:
            t = self.resw.get(r)
            if t is not None:
                deps.append(t)
        for w in writes:
            t = self.resw.get(w)
            if t is not None:
                deps.append(t)
            deps.extend(self.resr.get(w, ()))
        return deps

    def _waits(self, eng, deps):
        kn = self.known[eng]
        best = {}
        for (sk, val, clock) in deps:
            if kn.get(sk, 0) >= val:
                continue
            if best.get(sk, 0) < val:
                best[sk] = val
            for k2, v2 in clock.items():
                if kn.get(k2, 0) < v2:
                    kn[k2] = v2
            kn[sk] = val
        return list(best.items())

    def _commit(self, token, reads, writes):
        for w in writes:
            self.resw[w] = token
            self.resr[w] = []
        for r in reads:
            if r in writes:
                continue
            self.resr.setdefault(r, []).append(token)

    def emit(self, eng, fn, reads=(), writes=()):
        pr = [r for r in reads if (r == "ps7" or (isinstance(r, tuple) and r[0] == "ps")) and r not in writes]
        deps = self._deps([r for r in reads if r not in pr], writes)
        for r in pr:
            lw = self.resw.get(r)
            if lw is not None:
                deps.append(lw)
            for t in self.resr.get(r, ()):
                if t[0] != eng:
                    deps.append(t)
        waits = self._waits(eng, deps)
        self.cnt[eng] += 1
        val = self.cnt[eng]
        token = (eng, val, dict(self.known[eng]))
        if eng == "pe":
            self.known[eng][eng] = val
        rec = _Rec()
        fn(rec)
        calls = rec.calls
        assert calls

        def fn2(e, calls=calls):
            for name, a, k in calls:
                ins = getattr(e, name)(*a, **k)
            return ins
        self.streams[eng].append((waits, fn2, (eng, 1)))
        self._commit(token, reads, writes)
        return token

    def dma(self, eng, semkey, out, in_, reads=(), writes=()):
        if semkey not in self.dsem:
            self.dsem[semkey] = self.es.enter_context(self.nc.semaphore("d_" + semkey))
            self.dcnt[semkey] = 0
        waits = self._waits(eng, self._deps(reads, writes))
        self.dcnt[semkey] += 16
        val = self.dcnt[semkey]
        token = ("D:" + semkey, val, dict(self.known[eng]))

        def fn(e, out=out, in_=in_):
            return e.dma_start(out=out, in_=in_)
        self.streams[eng].append((waits, fn, ("D:" + semkey, 16)))
        self._commit(token, reads, writes)
        return token

    def semh(self, sk):
        if sk.startswith("D:"):
            return self.dsem[sk[2:]]
        return self.sem[sk]

    def barrier(self, engs=("pe", "act", "dve", "pool", "sp"), with_dma=True):
        allw = [(e, self.cnt[e]) for e in self.ENG if self.cnt[e] > 0]
        if with_dma:
            allw += [("D:" + k, v) for k, v in self.dcnt.items()]
        else:
            allw = [(e, v) for e, v in allw if e in engs]
        for eng in engs:
            kn = self.known[eng]
            waits = []
            for sk, val in allw:
                if sk == eng and eng in ("pe", "sp"):
                    continue
                if kn.get(sk, 0) < val:
                    waits.append((sk, val))
                    kn[sk] = val
            if waits:
                self.streams[eng].append((waits, None, None))

    def replay(self):
        nc = self.nc
        with nc.Block() as block:
            def mk(ename):
                def body(e):
                    for waits, fn, inc in self.streams[ename]:
                        for sk, val in waits:
                            e.wait_ge(self.semh(sk), val)
                        if fn is not None:
                            ins = fn(e)
                            ins.then_inc(self.semh(inc[0]), inc[1])
                return body
            block.tensor(mk("pe"))
            block.scalar(mk("act"))
            block.vector(mk("dve"))
            block.gpsimd(mk("pool"))
            block.sync(mk("sp"))


class Arena:
    def __init__(self, ap, nwords):
        self.ap = ap
        self.n = nwords
        self.off = 0
        self.base = 0
        self.peak = 0

    def f32(self, n):
        a = self.off
        self.off += n
        self.peak = max(self.peak, self.off)
        assert self.off <= self.n, ("arena overflow", self.off, self.n)
        return self.ap[:, a:a + n]

    def bf(self, n):
        w = (n + 1) // 2
        return self.f32(w).bitcast(BF16)[:, 0:n]

    def mark(self):
        self.base = self.off

    def reset(self):
        self.off = self.base


def _pp_layout():
    lay = {}
    off = 0
    for name, n in [("ident", 128), ("tle", 128), ("tgt", 128), ("tge", 128), ("tlt", 128), ("ones", 128),
                    ("pcol", 1), ("rowv", 8), ("condT", 16), ("adab0T", 16), ("adab1T", 16),
                    ("aconvw", 64), ("aconvb", 16), ("bconvw", 248), ("bconvb", 8), ("blng", 8), ("blnb", 8),
                    ("clng", 16), ("clnb", 16)]:
        lay[name] = (off, n)
        off += n
    return lay, off


PPL, NPP = _pp_layout()


def _rows_layout():
    lay = {}
    off = 0
    for name, n in [("qidx", 256), ("adab0g", 1024), ("adab1g", 1024), ("lng0", 1024), ("lnb0", 1024),
                    ("lng1", 1024), ("lnb1", 1024), ("anormw", 1024), ("dtbias", 32), ("alog", 32), ("ad", 16),
                    ("cbs", 2048)]:
        lay[name] = (off, n)
        off += n
    return lay, off


RWL, NRW = _rows_layout()
NW = 53100


def build_nc():
    nc = bass.Bass("TRN2", target_bir_lowering=False)
    din = lambda name, shape: nc.dram_tensor(name, shape, F32, kind="ExternalInput").ap()
    xin = din("xin", [NTOK, D])
    pp_d = din("pp", [128, NPP])
    rows_d = din("rows", [1, NRW])
    h0f_d = din("h0f", [1024, 128])
    h0b_d = din("h0b", [1024, 128])
    wsT_d = din("wsT", [128, 2048])
    ada_d = [din("ada0", [1024, 3072]), din("ada1", [1024, 3072])]
    win0 = din("win0", [1024, 6176])
    wout0 = din("wout0", [2048, 1024])
    win1 = din("win1", [1024, 6144])
    wout1 = din("wout1", [2048, 1024])
    yout = nc.dram_tensor("yout", [NTOK, D], F32, kind="ExternalOutput").ap()
    hfo = nc.dram_tensor("hfo", [2, 1024, 128], F32, kind="ExternalOutput").ap()
    hbo = nc.dram_tensor("hbo", [2, 1024, 128], F32, kind="ExternalOutput").ap()

    with ExitStack() as es:
        kb = KB(nc, es)
        arena_t = es.enter_context(nc.sbuf_tensor("arena", [128, NW], F32))
        A = Arena(arena_t, NW)
        psh = [es.enter_context(nc.psum_tensor("ps%d" % i, [128, 512], F32)) for i in range(8)]
        ps = [h[:, :] for h in psh]
        psb = [h.bitcast(BF16)[:, :] for h in psh]
        rot = [0]

        def pbank():
            b = rot[0]
            rot[0] = (rot[0] + 1) % 7
            return b

        def E(eng, f, r=(), w=()):
            return kb.emit(eng, f, reads=r, writes=w)

        resid = A.f32(NT * D).rearrange("p (t d) -> p t d", t=NT)
        hT = A.bf(8 * NTOK).rearrange("p (k t) -> p k t", k=8)
        wslot = [A.bf(8 * 512).rearrange("p (k n) -> p k n", k=8) for _ in range(2)]
        pp = A.f32(NPP)
        gate_b = A.f32(2 * D).rearrange("p (r d) -> p r d", r=2)
        cb16 = {n: A.bf(128) for n in ("ident", "tle", "tgt", "tge", "tlt", "ones")}
        modT = A.f32(32).rearrange("p (j r) -> p j r", r=2)
        opsc = A.f32(16).rearrange("p (j r) -> p j r", r=2)
        sc32 = A.f32(16)
        scb = A.bf(16)
        epst = A.f32(4)
        pp_eps_ln = epst[:, 0:1]
        pp_eps_rms = epst[:, 1:2]
        pp_one = epst[:, 2:3]
        E("dve", lambda e: e.memset(epst[:, 0:1], LN_EPS), [], ["eps"])
        E("dve", lambda e: e.memset(epst[:, 1:2], RMS_EPS), [], ["eps"])
        E("dve", lambda e: e.memset(epst[:, 2:3], 1.0), [], ["eps"])
        A.mark()

        def P(name, a=0, b=None):
            o, n = PPL[name]
            if b is None:
                b = n
            return pp[:, o + a:o + b]

        def rowb(name, a=0, b=None):
            o, n = RWL[name]
            if b is None:
                b = n
            return rows_d[0:1, o + a:o + b].broadcast_to([128, b - a])

        slot_i = [0]

        def load_w(parts):
            s = slot_i[0]
            slot_i[0] = (s + 1) % 2
            names = []
            for i, (src, kt0, col0) in enumerate(parts):
                rows, n = src.shape
                nk = rows // 128
                rn = ("w", s, i)
                kb.dma("pool", "w%d_%d" % (s, i), wslot[s][:, kt0:kt0 + nk, col0:col0 + n],
                       src.rearrange("(k p) n -> p k n", p=128),
                       writes=[("w", s, j) for j in range(4)] if i == 0 else [rn])
                names.append(rn)
            return wslot[s], [("w", s, j) for j in range(4)]

        import os
        kstop = int(os.environ.get("KSTOP", "99"))
        stage = [0]

        class _Stop(Exception):
            pass

        def phase(pool=True):
            kb.barrier(("pe", "act", "dve", "pool", "sp") if pool else ("pe", "act", "dve", "sp"))
            A.reset()
            stage[0] += 1
            print("stage", stage[0])
            if stage[0] > kstop:
                raise _Stop()

        kb.dma("sp", "pp", pp, pp_d, writes=["pp"])
        for n in cb16:
            E("dve", lambda e, n=n: e.tensor_copy(cb16[n], P(n)), ["pp"], [("c16", n)])
        C16 = [("c16", n) for n in cb16]

        ada_bufs = {}

        def adaln(L, do_phase=True, part="ab"):
            deferred = []
            if do_phase:
                phase()
            if "a" in part:
                bcl = [[A.bf(128) for r in range(2)] for k in range(8)]
                adabg = A.f32(1024)
                ada_bufs[L] = (bcl, adabg)
                kb.dma("sp", "adabg", adabg, rowb("adab%dg" % L), writes=["adabg"])
                E("act", lambda e: e.activation(sc32, P("condT"), AF.Silu), ["pp"], ["sc32"])
                E("dve", lambda e: e.tensor_copy(scb, sc32), ["sc32"], ["scb"])
                for k in range(8):
                    for r in range(2):
                        E("dve", lambda e, k=k, r=r: e.tensor_scalar(bcl[k][r], cb16["ones"], sc32[:, 2 * k + r:2 * k + r + 1], None, ALU.mult),
                          ["sc32"] + C16, [("bcl", k, r)])
            bcl, adabg = ada_bufs[L]
            scbv = scb.rearrange("p (k r) -> p k r", r=2)
            for blk in (range(6) if part == "ab" else (range(4) if part == "a" else range(4, 6))):
                slot, wn = load_w([(ada_d[L][:, blk * 512:(blk + 1) * 512], 0, 0)])
                if blk < 4:
                    for j in range(4):
                        jj = blk * 4 + j

                        def f(e, slot=slot, j=j, jj=jj):
                            for k in range(8):
                                ins = e.matmul(ps[7][:, jj * 2:jj * 2 + 2], slot[:, k, j * 128:(j + 1) * 128], scbv[:, k, :],
                                               start=(k == 0), stop=(k == 7))
                            return ins
                        E("pe", f, wn + ["scb"], ["ps7"])
                else:
                    nh = blk - 4
                    for r in range(2):
                        b = pbank()

                        def f(e, slot=slot, r=r, b=b):
                            for k in range(8):
                                ins = e.matmul(ps[b], bcl[k][r], slot[:, k, :], start=(k == 0), stop=(k == 7))
                            return ins
                        E("pe", f, wn + [("bcl", k, r) for k in range(8)], [("ps", b)])
                        deferred.append(lambda r=r, b=b, nh=nh: E("dve", lambda e: e.tensor_tensor(gate_b[:, r, nh * 512:(nh + 1) * 512], ps[b], adabg[:, nh * 512:(nh + 1) * 512], ALU.add),
                                                                    [("ps", b), "adabg"], [("gate", r)]))
                if blk == 3:
                    def _mod():
                        E("dve", lambda e: e.tensor_tensor(modT, ps[7][:, 0:32].rearrange("p (j r) -> p j r", r=2),
                                                           P("adab%dT" % L).unsqueeze(2).broadcast_to([128, 16, 2]), ALU.add),
                          ["ps7", "pp"], ["modT"])
                        E("dve", lambda e: e.tensor_scalar(opsc, modT[:, 8:16, :], 1.0, None, ALU.add), ["modT"], ["opsc"])
                    deferred.append(_mod)
            return deferred

        def pos_embed():
            freqB = A.f32(256)
            posC = A.f32(512)
            kb.dma("sp", "q", freqB, rowb("qidx"), writes=["freq"])
            E("act", lambda e: e.activation(freqB, freqB, AF.Exp, scale=-math.log(10000.0) / 256.0), ["freq"], ["freq"])

            def sincos(scal, tag):
                argt = A.f32(512)
                kint = A.f32(512).bitcast(I32)
                mred = A.f32(512)
                dst = posC if tag == "c" else A.f32(512)
                E("dve", lambda e: e.tensor_scalar(argt[:, 0:256], freqB, scal, None, ALU.mult), ["freq", "pp"], [("argt", tag)])
                E("dve", lambda e: e.tensor_scalar(argt[:, 256:512], argt[:, 0:256], math.pi / 2, None, ALU.add), [("argt", tag)], [("argt", tag)])
                E("dve", lambda e: e.tensor_scalar(kint, argt, 1.0 / TWO_PI, None, ALU.mult), [("argt", tag)], [("kint", tag)])
                E("dve", lambda e: e.scalar_tensor_tensor(mred, kint, -TWO_PI, argt, ALU.mult, ALU.add), [("kint", tag), ("argt", tag)], [("mred", tag)])
                E("act", lambda e: e.activation(dst, mred, AF.Sin), [("mred", tag)], [("pos", tag)])
                return dst

            sincos(P("pcol"), "c")
            posRs = [sincos(P("rowv", s, s + 1), s) for s in range(8)]
            return posRs, posC

        def pos_add(posRs, posC):
            for s in range(8):
                t = 4 + s
                E("dve", lambda e, t=t, s=s: e.tensor_tensor(resid[:, t, 0:512], resid[:, t, 0:512], posRs[s], ALU.add),
                  [("res", t), ("pos", s)], [("res", t)])
                E("dve", lambda e, t=t: e.tensor_tensor(resid[:, t, 512:1024], resid[:, t, 512:1024], posC, ALU.add),
                  [("res", t), ("pos", "c")], [("res", t)])


        def transposes(tiles=range(NT)):
            for t in tiles:
                r = cond_of(t)
                for half in range(2):
                    b = pbank()

                    def f(e, t=t, half=half, b=b):
                        for j in range(4):
                            k = half * 4 + j
                            ins = e.transpose(ps[b][:, j * 128:(j + 1) * 128], resid[:, t, k * 128:(k + 1) * 128], P("ident"))
                        return ins
                    E("pe", f, [("res", t), "pp"], [("ps", b)])
                    for j in range(4):
                        k = half * 4 + j
                        if half == 0:
                            E("dve", lambda e, t=t, k=k, j=j, b=b, r=r: e.tensor_scalar(
                                hT[:, k, t * 128:(t + 1) * 128], ps[b][:, j * 128:(j + 1) * 128],
                                opsc[:, k, r:r + 1], modT[:, k, r:r + 1], ALU.mult, ALU.add),
                              [("ps", b), "opsc", "modT"], [("hT", t)])
                        else:
                            E("act", lambda e, t=t, k=k, j=j, b=b, r=r: e.activation(
                                hT[:, k, t * 128:(t + 1) * 128], ps[b][:, j * 128:(j + 1) * 128], AF.Identity,
                                bias=modT[:, k, r:r + 1], scale=opsc[:, k, r:r + 1]),
                              [("ps", b), "opsc", "modT"], [("hT", t)])

        HT_ALL = [("hT", t) for t in range(NT)]

        def hT_of_tb(tb):
            return [("hT", t) for t in range(tb * 4, tb * 4 + 4)]

        def outproj_partial(lhs_fn, lhs_res_fn, nkt, w_d, row0, first, tmp):
            for nh in range(2):
                parts = []
                done = 0
                slots = []
                while done < nkt:
                    n = min(8, nkt - done)
                    slot, wn = load_w([(w_d[row0 + done * 128:row0 + (done + n) * 128, nh * 512:(nh + 1) * 512], 0, 0)])
                    slots.append((slot, wn, done, n))
                    done += n
                pend_acc = []
                for c in range(NT):
                    r = cond_of(c)
                    b = pbank()

                    def f(e, c=c, b=b, slots=slots):
                        i = 0
                        for slot, wn, k0, n in slots:
                            for kk in range(n):
                                ins = e.matmul(ps[b], lhs_fn(k0 + kk, c), slot[:, kk, :], start=(i == 0), stop=(i == nkt - 1))
                                i += 1
                        return ins
                    rd = []
                    for slot, wn, k0, n in slots:
                        rd += wn
                    E("pe", f, rd + lhs_res_fn(c), [("ps", b)])
                    tv, tn = tmp[c % len(tmp)]
                    E("dve", lambda e, b=b, r=r, nh=nh, tv=tv: e.tensor_tensor(tv, ps[b], gate_b[:, r, nh * 512:(nh + 1) * 512], ALU.mult),
                      [("ps", b), ("gate", r)], [tn])
                    rs = resid[:, c, nh * 512:(nh + 1) * 512]

                    def _acc(rs=rs, tv=tv, tn=tn, c=c):
                        if first:
                            E("dve", lambda e: e.scalar_tensor_tensor(rs, rs, ALPHA, tv, ALU.mult, ALU.add), [tn, ("res", c)], [("res", c)])
                        else:
                            E("dve", lambda e: e.tensor_tensor(rs, rs, tv, ALU.add), [tn, ("res", c)], [("res", c)])
                    if pend_acc:
                        pend_acc.pop()()
                    pend_acc.append(_acc)
                if pend_acc:
                    pend_acc.pop()()

        def deepnorm_ln(L, do_phase=True):
            if do_phase:
                phase()
            lng = A.f32(1024)
            lnb = A.f32(1024)
            tmp = [A.f32(1024), A.f32(1024)]
            s1 = A.f32(12)
            s2 = A.f32(12)
            mean = A.f32(12)
            msq = A.f32(12)
            rstd = A.f32(12)
            nmr = A.f32(12)
            kb.dma("sp", "lng", lng, rowb("lng%d" % L), writes=["lng"])
            kb.dma("sp", "lnb", lnb, rowb("lnb%d" % L), writes=["lnb"])
            E("dve", lambda e: e.memset(s1, 0.0), [], ["s1"])
            E("dve", lambda e: e.memset(s2, 0.0), [], ["s2"])
            for c in range(NT):
                q = c % 2
                E("act", lambda e, c=c, q=q: e.activation(tmp[q], resid[:, c, :], AF.Copy, accum_out=s1[:, c:c + 1]), [("res", c), "s1"], [("lntmp", q), ("s1", c)])
                E("act", lambda e, c=c, q=q: e.activation(tmp[q], resid[:, c, :], AF.Square, accum_out=s2[:, c:c + 1]), [("res", c), "s2"], [("lntmp", q), ("s2", c)])
            SA = [("s1", c) for c in range(NT)] + [("s2", c) for c in range(NT)] + ["s1", "s2"]
            E("dve", lambda e: e.tensor_scalar(mean, s1, 1.0 / D, None, ALU.mult), SA, ["mean"])
            E("dve", lambda e: e.tensor_tensor(msq, mean, mean, ALU.mult), ["mean"], ["msq"])
            E("dve", lambda e: e.scalar_tensor_tensor(rstd, s2, 1.0 / D, msq, ALU.mult, ALU.subtract), SA + ["msq"], ["rstd"])
            E("act", lambda e: e.activation(rstd, rstd, AF.Ln, bias=pp_eps_ln), ["rstd", "eps"], ["rstd"])
            E("act", lambda e: e.activation(rstd, rstd, AF.Exp, scale=-0.5), ["rstd"], ["rstd"])
            E("dve", lambda e: e.scalar_tensor_tensor(nmr, mean, -1.0, rstd, ALU.mult, ALU.mult), ["mean", "rstd"], ["nmr"])
            for c in range(NT):
                q = c % 2
                E("act", lambda e, c=c, q=q: e.activation(tmp[q], resid[:, c, :], AF.Identity, bias=nmr[:, c:c + 1], scale=rstd[:, c:c + 1]),
                  [("res", c), "rstd", "nmr"], [("lntmp", q)])
                E("dve", lambda e, c=c, q=q: e.tensor_tensor(tmp[q], tmp[q], lng, ALU.mult), [("lntmp", q), "lng"], [("lntmp", q)])
                E("dve", lambda e, c=c, q=q: e.tensor_tensor(resid[:, c, :], tmp[q], lnb, ALU.add), [("lntmp", q), "lnb"], [("res", c)])

        def _body():
            pos_tabs = pos_embed()
            dfr0 = adaln(0, do_phase=False, part="a")
            for t in range(NT):
                kb.dma("pool", "x%d" % t, resid[:, t, :], xin[t * 128:(t + 1) * 128, :], writes=[("res", t)])
            for d_ in dfr0:
                d_()
            transposes(range(0, 4))
            pos_add(*pos_tabs)
            transposes(range(4, NT))
            for d_ in adaln(0, do_phase=False, part="b"):
                d_()

            phase(pool=False)
            sgT = A.bf(8 * NTOK).rearrange("p (k t) -> p k t", k=8)
            convout = A.f32(8 * NTOK).rearrange("p (k t) -> p k t", k=8)
            NPE = 23
            upad2 = [A.bf(UL + 2), A.bf(UL + 2)]
            accB = A.f32(UL + 2)
            diag = A.bf(NPE * 128).rearrange("p (k c) -> p k c", k=NPE)
            bwh = A.f32(248)
            tsA = A.f32(512)
            tsB = A.f32(512)
            tsC = A.f32(512)
            tsD = A.f32(512)
            tsQ = [A.f32(512), A.f32(512)]
            tsSet = [(tsB, tsC, tsD), (tsB, tsC, tsD)]
            for q_ in range(2):
                E("dve", lambda e, q_=q_: e.memset(upad2[q_], 0.0), [], [("upad", q_)])
            E("dve", lambda e: e.tensor_scalar(bwh, P("bconvw"), 0.5, None, ALU.mult), ["pp"], ["bwh"])
            VAL0, GLU0, GAT0 = 3104, 4128, 5152
            for ct in range(8):
                upad = upad2[ct % 2]
                upn = ("upad", ct % 2)
                slot, wn = load_w([(win0[:, VAL0 + ct * 128:VAL0 + (ct + 1) * 128], 0, 0),
                                   (win0[:, GLU0 + ct * 128:GLU0 + (ct + 1) * 128], 0, 128),
                                   (win0[:, GAT0 + ct * 128:GAT0 + (ct + 1) * 128], 0, 256)])
                E("dve", lambda e, ct=ct: e.tensor_tensor(diag, cb16["ident"].unsqueeze(1).broadcast_to([128, NPE, 128]),
                                                          bwh[:, ct * 31:ct * 31 + NPE].unsqueeze(2).broadcast_to([128, NPE, 128]), ALU.mult),
                  C16 + ["bwh"], ["diag"])
                for tb in range(3):
                    bs_ = []
                    for j in range(3):
                        b = pbank()
                        bs_.append(b)

                        def f(e, slot=slot, j=j, tb=tb, b=b):
                            for k in range(8):
                                ins = e.matmul(ps[b], slot[:, k, j * 128:(j + 1) * 128], hT[:, k, tb * 512:(tb + 1) * 512],
                                               start=(k == 0), stop=(k == 7))
                            return ins
                        E("pe", f, wn + hT_of_tb(tb), [("ps", b)])
                    bv, bg, bt = bs_
                    E("act", lambda e, bg=bg: e.activation(tsA, ps[bg], AF.Tanh, scale=0.5), [("ps", bg)], ["tsA"])
                    if tb == 0:
                        segs = [(UOFF[0], 0, 256), (UOFF[1], 256, 256)]
                    else:
                        segs = [(UOFF[2] + (tb - 1) * 512, 0, 512)]
                    for (uo, so, n) in segs:
                        E("dve", lambda e, uo=uo, so=so, n=n, bv=bv: e.scalar_tensor_tensor(
                            upad[:, uo:uo + n], tsA[:, so:so + n], 1.0, ps[bv][:, so:so + n], ALU.add, ALU.mult),
                          ["tsA", ("ps", bv)], [upn])
                    E("act", lambda e, bt=bt, ct=ct, tb=tb: e.activation(sgT[:, ct, tb * 512:(tb + 1) * 512], ps[bt], AF.Silu),
                      [("ps", bt)], [("sgT", ct, tb)])
                LCB = UL - 30
                HB = LCB // 2
                for k in range(NPE, 31):
                    wk = bwh[:, ct * 31 + k:ct * 31 + k + 1]
                    for hname, a0, a1 in (("accB0", 0, HB), ("accB1", HB, LCB)):
                        if k == NPE:
                            E("dve", lambda e, wk=wk, k=k, a0=a0, a1=a1: e.tensor_scalar(accB[:, a0:a1], upad[:, k + a0:k + a1], wk, None, ALU.mult),
                              [upn, "bwh"], [hname])
                        else:
                            E("dve", lambda e, wk=wk, k=k, a0=a0, a1=a1: e.scalar_tensor_tensor(accB[:, a0:a1], upad[:, k + a0:k + a1], wk, accB[:, a0:a1],
                                                                                               ALU.mult, ALU.add),
                              [upn, "bwh", hname], [hname])
                for (base, tok0, n) in [(0, 0, 256), (286, 256, 256), (572, 512, 512), (572 + 512, 1024, 512)]:
                    b = pbank()

                    def f(e, base=base, n=n, b=b):
                        for k in range(NPE):
                            ins = e.matmul(ps[b][:, 0:n], diag[:, k, :], upad[:, base + k:base + k + n], start=(k == 0), stop=(k == NPE - 1))
                        return ins
                    E("pe", f, ["diag", upn], [("ps", b)])
                    E("dve", lambda e, b=b, n=n, tok0=tok0, ct=ct, base=base: e.scalar_tensor_tensor(
                        convout[:, ct, tok0:tok0 + n], ps[b][:, 0:n], P("bconvb", ct, ct + 1), accB[:, base:base + n], ALU.add, ALU.add),
                      [("ps", b), "pp", "accB0", "accB1"], [("conv", ct)])
            for tb in range(3):
                tsl = slice(tb * 512, (tb + 1) * 512)
                mB, mC, mD = tsSet[tb % 2]
                nB, nC, nD = "tsB", "tsC", "tsD"
                b1 = pbank()
                b2 = pbank()

                def f(e, tsl=tsl, b1=b1):
                    for ct in range(8):
                        ins = e.matmul(ps[b1], P("ones"), convout[:, ct, tsl], start=(ct == 0), stop=(ct == 7))
                    return ins
                E("pe", f, [("conv", ct) for ct in range(8)] + ["pp"], [("ps", b1)])
                for ct in range(8):
                    q = ct % 2
                    E("act", lambda e, ct=ct, tsl=tsl, q=q: e.activation(tsQ[q], convout[:, ct, tsl], AF.Square), [("conv", ct)], [("tsQ", q)])
                    E("pe", lambda e, ct=ct, b2=b2, q=q: e.matmul(ps[b2], P("ones"), tsQ[q], start=(ct == 0), stop=(ct == 7)), [("tsQ", q), "pp"], [("ps", b2)])
                E("dve", lambda e, b1=b1: e.tensor_scalar(mB, ps[b1], 1.0 / 1024, None, ALU.mult), [("ps", b1)], [nB])
                E("dve", lambda e: e.tensor_tensor(mD, mB, mB, ALU.mult), [nB], [nD])
                E("dve", lambda e, b2=b2: e.scalar_tensor_tensor(mC, ps[b2], 1.0 / 1024, mD, ALU.mult, ALU.subtract), [("ps", b2), nD], [nC])
                E("act", lambda e: e.activation(mC, mC, AF.Ln, bias=pp_eps_ln), [nC, "eps"], [nC])
                E("act", lambda e: e.activation(mC, mC, AF.Exp, scale=-0.5), [nC], [nC])
                E("dve", lambda e: e.scalar_tensor_tensor(mD, mB, -1.0, mC, ALU.mult, ALU.mult), [nB, nC], [nD])
                for ct in range(8):
                    q = ct % 2
                    tq, tqn = (tsA, "tsA") if q == 0 else (tsQ[0], ("tsQ", 0))
                    E("dve", lambda e, ct=ct, tsl=tsl, tq=tq: e.tensor_tensor(tq, convout[:, ct, tsl], mC, ALU.mult), [("conv", ct), nC], [tqn])
                    E("dve", lambda e, tq=tq: e.tensor_tensor(tq, tq, mD, ALU.add), [tqn, nD], [tqn])
                    E("act", lambda e, ct=ct, tq=tq: e.activation(tq, tq, AF.Silu, bias=P("blnb", ct, ct + 1), scale=P("blng", ct, ct + 1)), [tqn, "pp"], [tqn])
                    E("dve", lambda e, ct=ct, tsl=tsl, tq=tq: e.tensor_tensor(sgT[:, ct, tsl], tq, sgT[:, ct, tsl], ALU.mult),
                      [tqn, ("sgT", ct, tb)], [("sgT", ct, tb)])
            outproj_partial(lambda kt, c: sgT[:, kt, c * 128:(c + 1) * 128],
                            lambda c: [("sgT", ct, c // 4) for ct in range(8)], 8, wout0, 1024, True, [(tsB, "tsB"), (tsC, "tsC"), (tsD, "tsD")])

            phase(pool=False)
            x_tok = A.bf(NT * 256).rearrange("p (c d) -> p c d", c=NT)
            xD = A.bf(NT * 256).rearrange("p (c d) -> p c d", c=NT)
            B_tok = A.bf(NT * 128).rearrange("p (c d) -> p c d", c=NT)
            BT = A.bf(NTOK)
            CT = A.bf(NTOK)
            zs = A.f32(NT * 256).rearrange("p (c d) -> p c d", c=NT)
            gin = A.bf(NT * 256).rearrange("p (c d) -> p c d", c=NT)
            yaT = A.bf(4 * NTOK).rearrange("p (k t) -> p k t", k=4)
            dtall = A.f32(NT * 32).rearrange("p (c h) -> p c h", c=NT)
            dtA = A.f32(NT * 32).rearrange("p (c h) -> p c h", c=NT)
            Etab = A.f32(NT * 160).rearrange("p (c m h) -> p c m h", c=NT, m=5)
            wdd = A.f32(NT * 32).rearrange("p (c h) -> p c h", c=NT)
            dtbias_b = A.f32(32)
            aneg_b = A.f32(32)
            ad_b = A.f32(16)
            normw_g = A.f32(256)
            f_state = A.f32(256)
            g_state = A.f32(256)
            tF = A.f32(256)
            ssrT = A.f32(2 * NT)
            h0ld = A.f32(256).rearrange("p (i n) -> p i n", i=2)
            sto = h0ld
            scratch0 = A.off
            rawpad = [A.bf(RL + 3), A.bf(RL + 3)]
            diagA = A.bf(16 * 128).rearrange("p (k c) -> p k c", k=16)
            xaT = [A.bf(NTOK), A.bf(NTOK)]
            A.off = scratch0
            V2 = lambda mk: [mk(), mk()]
            NEGM = {"f": A.bf(128), "b": A.bf(128)}
            Xf = V2(lambda: A.f32(512))
            Xb = V2(lambda: A.f32(512))
            Lf = V2(lambda: A.f32(512))
            Lb = V2(lambda: A.f32(512))
            MTf = V2(lambda: A.bf(512).rearrange("p (r i) -> p r i", r=4))
            MTb = V2(lambda: A.bf(512).rearrange("p (r i) -> p r i", r=4))
            xdtf = V2(lambda: A.bf(256))
            xdtb = V2(lambda: A.bf(256))
            xdd = V2(lambda: A.bf(256))
            Ssb = [A.f32(256), A.f32(256), A.f32(256)]
            fin = V2(lambda: A.bf(256))
            t1 = V2(lambda: A.f32(256))
            t2 = V2(lambda: A.f32(256))
            yv = V2(lambda: A.f32(256))
            ynb = V2(lambda: A.bf(256))
            ssr = V2(lambda: A.f32(2))
            optmp = [(Xf[0], ("X", "f", 0)), (Xf[1], ("X", "f", 1)), (Xb[0], ("X", "b", 0))]

            kb.dma("sp", "r1", dtbias_b, rowb("dtbias"), writes=["dtbias"])
            kb.dma("sp", "r2", aneg_b, rowb("alog"), writes=["aneg"])
            kb.dma("sp", "r3", ad_b, rowb("ad"), writes=["ad"])
            E("act", lambda e: e.activation(aneg_b, aneg_b, AF.Exp), ["aneg"], ["aneg"])
            E("dve", lambda e: e.tensor_scalar(aneg_b, aneg_b, -1.0, None, ALU.mult), ["aneg"], ["aneg"])
            masks32 = ["tle", "tgt", "ones", "tge", "tlt"]

            def bc_p(ap4):
                return ap4.unsqueeze(2).broadcast_to([128, 4, 64])

            def v464(ap):
                return ap.rearrange("p (r q) -> p r q", r=4)

            for g in range(4):
                kb.barrier(("pe", "act", "dve"), with_dma=False)
                for q_ in range(2):
                    E("dve", lambda e, q_=q_: e.memset(rawpad[q_], 0.0), [], [("rawpad", q_)])
                kb.dma("sp", "r4", normw_g, rowb("anormw", g * 256, (g + 1) * 256), writes=["normw"])
                S1, wn1 = load_w([(win0[:, 1024 + g * 256:1024 + g * 256 + 128], 0, 0),
                                  (win0[:, 1024 + g * 256 + 128:1024 + (g + 1) * 256], 0, 128),
                                  (win0[:, 2048 + g * 128:2048 + (g + 1) * 128], 0, 256),
                                  (win0[:, 2560 + g * 128:2560 + (g + 1) * 128], 0, 384)])
                S2, wn2 = load_w([(win0[:, g * 256:g * 256 + 128], 0, 0), (win0[:, g * 256 + 128:(g + 1) * 256], 0, 128),
                                  (win0[:, 3072:3104], 0, 256)])
                if g == 0:
                    for c in range(NT):
                        def f(e, c=c):
                            for k in range(8):
                                ins = e.matmul(ps[7][:, c * 32:(c + 1) * 32], hT[:, k, c * 128:(c + 1) * 128], S2[:, k, 256:288],
                                               start=(k == 0), stop=(k == 7))
                            return ins
                        E("pe", f, wn2 + [("hT", c)], ["ps7"])
                    E("dve", lambda e: e.tensor_tensor(dtall, ps[7][:, 0:384].rearrange("p (c h) -> p c h", c=NT),
                                                       dtbias_b.unsqueeze(1).broadcast_to([128, NT, 32]), ALU.add), ["ps7", "dtbias"], ["dtall"])
                    E("act", lambda e: e.activation(dtall, dtall, AF.Exp), ["dtall"], ["dtall"])
                    E("act", lambda e: e.activation(dtall, dtall, AF.Ln, bias=pp_one), ["dtall", "eps"], ["dtall"])
                    E("dve", lambda e: e.tensor_tensor(dtA, dtall, aneg_b.unsqueeze(1).broadcast_to([128, NT, 32]), ALU.mult), ["dtall", "aneg"], ["dtA"])
                    for c in range(NT):
                        b = pbank()

                        def f(e, c=c, b=b):
                            for m, mn in enumerate(masks32):
                                ins = e.matmul(ps[b][:, m * 32:(m + 1) * 32], P(mn), dtA[:, c, :], start=True, stop=True)
                            return ins
                        E("pe", f, ["dtA", "pp"], [("ps", b)])
                        E("act", lambda e, c=c, b=b: e.activation(Etab[:, c, :, :], ps[b][:, 0:160].rearrange("p (m h) -> p m h", m=5), AF.Exp),
                          [("ps", b)], [("Etab", c)])
                    ETALL = [("Etab", c) for c in range(NT)]
                    E("dve", lambda e: e.tensor_tensor(wdd[:, :, 0:16], dtall[:, :, 0:16], Etab[:, :, 1, 0:16], ALU.mult), ["dtall"] + ETALL, ["wdd"])
                    E("dve", lambda e: e.tensor_tensor(wdd[:, :, 16:32], dtall[:, :, 16:32], Etab[:, :, 4, 16:32], ALU.mult), ["dtall"] + ETALL, ["wdd"])
                def a1_vars(jt):
                    wt = (g * 2 + jt) if jt < 2 else (8 + g if jt == 2 else 12 + g)
                    q = jt % 2
                    return wt, rawpad[q], None, xaT[q], ("rawpad", q), ("acc", q), ("xaT", q)

                for jt_ in range(4):
                    wt_ = a1_vars(jt_)[0]
                    E("dve", lambda e, jt_=jt_, wt_=wt_: e.tensor_tensor(diagA[:, jt_ * 4:(jt_ + 1) * 4, :], cb16["ident"].unsqueeze(1).broadcast_to([128, 4, 128]),
                                                                  P("aconvw", wt_ * 4, wt_ * 4 + 4).unsqueeze(2).broadcast_to([128, 4, 128]), ALU.mult),
                      C16 + ["pp"], [("diagA", jt_)])

                def a1_part1(jt):
                    wt, RP, AC, XA, rpn, acn, xan = a1_vars(jt)
                    for tb in range(3):
                        b = pbank()

                        def f(e, tb=tb, b=b):
                            for k in range(8):
                                ins = e.matmul(ps[b], S1[:, k, jt * 128:(jt + 1) * 128], hT[:, k, tb * 512:(tb + 1) * 512],
                                               start=(k == 0), stop=(k == 7))
                            return ins
                        E("pe", f, wn1 + hT_of_tb(tb), [("ps", b)])
                        if tb == 0:
                            E("act", lambda e, b=b: e.activation(RP[:, ROFF[0]:ROFF[0] + 256], ps[b][:, 0:256], AF.Copy), [("ps", b)], [rpn])
                            E("act", lambda e, b=b: e.activation(RP[:, ROFF[1]:ROFF[1] + 256], ps[b][:, 256:512], AF.Copy), [("ps", b)], [rpn])
                        else:
                            o = ROFF[2] + (tb - 1) * 512
                            E("act", lambda e, b=b, o=o: e.activation(RP[:, o:o + 512], ps[b], AF.Copy), [("ps", b)], [rpn])

                def a1_part2(jt):
                    wt, RP, AC, XA, rpn, acn, xan = a1_vars(jt)
                    dst = {0: XA, 1: XA, 2: BT, 3: CT}[jt]
                    dname = {0: xan, 1: xan, 2: "BT", 3: "CT"}[jt]
                    for (base, tok0, n) in [(ROFF[0] - 2, 0, 256), (ROFF[1] - 2, 256, 256), (ROFF[2] - 2, 512, 512), (ROFF[2] - 2 + 512, 1024, 512)]:
                        b = pbank()

                        def f(e, base=base, n=n, b=b):
                            for k in range(4):
                                ins = e.matmul(ps[b][:, 0:n], diagA[:, jt * 4 + k, :], RP[:, base + k:base + k + n], start=(k == 0), stop=(k == 3))
                            return ins
                        E("pe", f, [("diagA", jt), rpn], [("ps", b)])
                        E("act", lambda e, b=b, n=n, tok0=tok0: e.activation(dst[:, tok0:tok0 + n], ps[b][:, 0:n], AF.Silu, bias=P("aconvb", wt, wt + 1)),
                          [("ps", b), "pp"], [dname])
                    if jt < 3:
                        src = XA if jt < 2 else BT
                        for cq in range(3):
                            b = pbank()

                            def f(e, cq=cq, b=b):
                                for j in range(4):
                                    c = cq * 4 + j
                                    ins = e.transpose(psb[b][:, j * 128:(j + 1) * 128], src[:, c * 128:(c + 1) * 128], cb16["ident"])
                                return ins
                            E("pe", f, [dname] + C16, [("ps", b)])
                            pv = psb[b][:, 0:512].rearrange("p (j d) -> p j d", j=4)
                            if jt < 2:
                                E("dve", lambda e, cq=cq, pv=pv: e.tensor_copy(x_tok[:, cq * 4:cq * 4 + 4, jt * 128:(jt + 1) * 128], pv),
                                  [("ps", b)], ["x_tok"])
                            else:
                                E("dve", lambda e, cq=cq, pv=pv: e.tensor_copy(B_tok[:, cq * 4:cq * 4 + 4, :], pv), [("ps", b)], ["B_tok"])

                a1_part1(0)
                for jt in range(4):
                    if jt + 1 < 4:
                        a1_part1(jt + 1)
                    a1_part2(jt)
                hsl = slice(4 * g, 4 * g + 4)
                hslb = slice(16 + 4 * g, 16 + 4 * g + 4)
                E("dve", lambda e: e.tensor_tensor(xD.rearrange("p c (r q) -> p c r q", r=4), x_tok.rearrange("p c (r q) -> p c r q", r=4),
                                                   ad_b[:, hsl].unsqueeze(1).unsqueeze(3).broadcast_to([128, NT, 4, 64]), ALU.mult),
                  ["x_tok", "ad"], ["xD"])
                kb.barrier(("pe", "act", "dve"), with_dma=False)

                def z_pair(cp):
                    b = pbank()

                    def f(e, cp=cp, b=b):
                        for j in range(2):
                            c = cp * 2 + j
                            for k in range(8):
                                ins = e.matmul(ps[b][:, j * 256:(j + 1) * 256], hT[:, k, c * 128:(c + 1) * 128], S2[:, k, 0:256],
                                               start=(k == 0), stop=(k == 7))
                        return ins
                    E("pe", f, wn2 + [("hT", cp * 2), ("hT", cp * 2 + 1)], [("ps", b)])
                    E("act", lambda e, cp=cp, b=b: e.activation(zs[:, cp * 2:cp * 2 + 2, :], ps[b].rearrange("p (j d) -> p j d", j=2), AF.Silu),
                      [("ps", b)], ["zs"])

                def load_state(h0_d, st, stname):
                    kb.dma("sp", "h0", h0ld, h0_d[g * 256:(g + 1) * 256, :].rearrange("(i p) n -> p i n", p=128), writes=["h0ld"])
                    b = pbank()

                    def f(e, b=b):
                        for i in range(2):
                            ins = e.transpose(ps[b][:, i * 128:(i + 1) * 128], h0ld[:, i, :], P("ident"))
                        return ins
                    E("pe", f, ["h0ld", "pp"], [("ps", b)])
                    E("dve", lambda e, b=b: e.tensor_copy(st, ps[b][:, 0:256]), [("ps", b)], [stname])

                def store_state(st, stname, out_d, s):
                    b = pbank()

                    def f(e, b=b):
                        for i in range(2):
                            ins = e.transpose(ps[b][:, i * 128:(i + 1) * 128], st[:, i * 128:(i + 1) * 128], P("ident"))
                        return ins
                    E("pe", f, [stname, "pp"], [("ps", b)])
                    E("act", lambda e, b=b: e.activation(sto, ps[b][:, 0:256].rearrange("p (i n) -> p i n", i=2), AF.Copy), [("ps", b)], ["h0ld"])
                    kb.dma("sp", "so", out_d[s, g * 256:(g + 1) * 256, :].rearrange("(i p) n -> p i n", p=128), sto, reads=["h0ld"])

                order1 = []
                for s, (c0, ncs) in enumerate(SEQS):
                    for c in range(c0 + ncs - 1, c0 - 1, -1):
                        order1.append((s, c, c == c0 + ncs - 1, c == c0))

                def p1_pre(i):
                    s, c, first, last = order1[i]
                    if s == 2 and last:
                        return
                    v = i % 2
                    E("dve", lambda e: e.tensor_tensor(v464(xdd[v]), v464(x_tok[:, c, :]), bc_p(wdd[:, c, hslb]), ALU.mult),
                      ["x_tok", "wdd"], [("xdd", v)])
                    b = pbank()
                    E("pe", lambda e: e.matmul(ps[b][:, 0:256], B_tok[:, c, :], xdd[v], start=True, stop=True), ["B_tok", ("xdd", v)], [("ps", b)])
                    E("act", lambda e: e.activation(Ssb[v], ps[b][:, 0:256], AF.Copy), [("ps", b)], [("Ssb", v)])
                p1_pre(0)
                for i, (s, c, first, last) in enumerate(order1):
                    v = i % 2
                    if i % 2 == 0:
                        z_pair(i // 2)
                    has_chain = not (s == 2 and last)
                    if (not has_chain) and i + 1 < len(order1):
                        p1_pre(i + 1)
                    if first:
                        if s == 2:
                            load_state(h0b_d, g_state, "g_state")
                        else:
                            E("dve", lambda e: e.memset(g_state, 0.0), [], ["g_state"])
                    E("act", lambda e: e.activation(gin[:, c, :], g_state, AF.Copy), ["g_state"], [("gin", c)])
                    if not has_chain:
                        continue
                    E("dve", lambda e: e.tensor_tensor(v464(tF), v464(g_state), bc_p(Etab[:, c, 2, hslb]), ALU.mult), ["g_state"] + ETALL, ["tF"])
                    if i + 1 < len(order1):
                        p1_pre(i + 1)
                    E("dve", lambda e: e.tensor_tensor(g_state, tF, Ssb[v], ALU.add), ["tF", ("Ssb", v)], ["g_state"])
                    if last and s < 2:
                        store_state(g_state, "g_state", hbo, s)

                order2 = []
                for s, (c0, ncs) in enumerate(SEQS):
                    for c in range(c0, c0 + ncs):
                        order2.append((s, c, c == c0, c == c0 + ncs - 1))

                E("dve", lambda e: e.memset(ssrT, 0.0), [], [("ssrc", i_) for i_ in range(NT)])
                E("dve", lambda e: e.tensor_scalar(NEGM["f"], cb16["tgt"], -30000.0, None, ALU.mult), C16, [("NEGM", "f")])
                E("dve", lambda e: e.tensor_scalar(NEGM["b"], cb16["tlt"], -30000.0, None, ALU.mult), C16, [("NEGM", "b")])

                def p2_step(step):
                    ia, ib, ic = step, step - 1, step - 2
                    hasA, hasB, hasC = ia < n2, 0 <= ib < n2, 0 <= ic < n2
                    if hasA:
                        sA, cA, firstA, lastA = order2[ia]
                        vA = ia % 2
                        cslA = slice(cA * 128, (cA + 1) * 128)
                        xvA = v464(x_tok[:, cA, :])
                    if hasB:
                        sB, cB, firstB, lastB = order2[ib]
                        vB = ib % 2
                        cslB = slice(cB * 128, (cB + 1) * 128)
                    if hasC:
                        sC, cC, firstC, lastC = order2[ic]
                        vC = ic % 2
                        cslC = slice(cC * 128, (cC + 1) * 128)
                    if hasC:
                        ssc = ssrT[:, 2 * ic:2 * ic + 2]
                        sscn = ("ssrc", ic)
                        E("act", lambda e: e.activation(t1[vC], yv[vC], AF.Square, accum_out=ssc[:, 0:1]),
                          [("yv", vC), sscn, ("t1", vC)], [("t1", vC), sscn])
                        E("act", lambda e: e.activation(ssc[:, 1:2], ssc[:, 0:1], AF.Ln, bias=pp_eps_rms, scale=1.0 / 256), [sscn, "eps"], [sscn])
                        E("act", lambda e: e.activation(ssc[:, 1:2], ssc[:, 1:2], AF.Exp, scale=-0.5), [sscn], [sscn])
                    if hasB:
                        by, bof, bob = pbank(), pbank(), pbank()

                        def f(e):
                            e.matmul(ps[by][:, 0:256], cb16["ident"], xD[:, cB, :], start=True, stop=False)
                            for r in range(4):
                                e.matmul(ps[by][:, r * 64:(r + 1) * 64], MTf[vB][:, r, :], xdtf[vB][:, r * 64:(r + 1) * 64], start=False, stop=False)
                                ins = e.matmul(ps[by][:, r * 64:(r + 1) * 64], MTb[vB][:, r, :], xdtb[vB][:, r * 64:(r + 1) * 64], start=False, stop=(r == 3))
                            return ins
                        E("pe", f, C16 + ["xD", ("MT", "f", vB), ("MT", "b", vB), ("xdtf", vB), ("xdtb", vB)], [("ps", by)])
                        E("pe", lambda e: e.matmul(ps[bof][:, 0:256], CT[:, cslB], fin[vB], start=True, stop=True), ["CT", ("fin", vB)], [("ps", bof)])
                        E("pe", lambda e: e.matmul(ps[bob][:, 0:256], CT[:, cslB], gin[:, cB, :], start=True, stop=True), ["CT", ("gin", cB)], [("ps", bob)])
                    if hasA:
                        segb_ = {}
                        for dr, msk, hs_, X in (("f", "tle", hsl, Xf), ("b", "tge", hslb, Xb)):
                            E("dve", lambda e: e.tensor_tensor(X[vA].rearrange("p (r i) -> p r i", r=4), P(msk).unsqueeze(1).broadcast_to([128, 4, 128]),
                                                               dtA[:, cA, hs_].unsqueeze(2).broadcast_to([128, 4, 128]), ALU.mult),
                              ["pp", "dtA"], [("X", dr, vA)])
                        bcb = pbank()
                        E("pe", lambda e: e.matmul(ps[bcb][:, 0:128], BT[:, cslA], CT[:, cslA], start=True, stop=True), ["BT", "CT"], [("ps", bcb)])
                        for dr, lhs, X in (("f", "tgt", Xf), ("b", "tlt", Xb)):
                            b = pbank()
                            segb_[dr] = b
                            def f(e, dr=dr, lhs=lhs, X=X, b=b):
                                e.matmul(ps[b].rearrange("p (r i) -> p r i", r=4), cb16["ident"], NEGM[dr].unsqueeze(1).broadcast_to([128, 4, 128]),
                                         start=True, stop=False)
                                return e.matmul(ps[b], P(lhs), X[vA], start=False, stop=True)
                            E("pe", f, C16 + ["pp", ("X", dr, vA), ("NEGM", dr)], [("ps", b)])
                    if hasC:
                        E("dve", lambda e: e.scalar_tensor_tensor(ynb[vC], yv[vC], ssc[:, 1:2], normw_g, ALU.mult, ALU.mult),
                          [("yv", vC), sscn, "normw"], [("ynb", vC)])
                        bt = pbank()

                        def f(e):
                            for j in range(2):
                                ins = e.transpose(psb[bt][:, j * 128:(j + 1) * 128], ynb[vC][:, j * 128:(j + 1) * 128], cb16["ident"])
                            return ins
                        E("pe", f, [("ynb", vC)] + C16, [("ps", bt)])
                    if hasB:
                        E("dve", lambda e: e.tensor_tensor(v464(t1[vB]), v464(ps[bof][:, 0:256]), bc_p(Etab[:, cB, 0, hsl]), ALU.mult), [("ps", bof)] + ETALL, [("t1", vB)])
                        for r in range(4):
                            E("act", lambda e, r=r: e.activation(t2[vB][:, r * 64:(r + 1) * 64], ps[bob][:, r * 64:(r + 1) * 64], AF.Copy,
                                                                 scale=Etab[:, cB, 3, 16 + 4 * g + r:16 + 4 * g + r + 1]), [("ps", bob)] + ETALL, [("t2", vB)])
                    if hasA:
                        E("dve", lambda e: e.tensor_tensor(v464(xdtf[vA]), xvA, bc_p(dtall[:, cA, hsl]), ALU.mult), ["x_tok", "dtall"], [("xdtf", vA)])
                        E("dve", lambda e: e.tensor_tensor(v464(xdtb[vA]), xvA, bc_p(dtall[:, cA, hslb]), ALU.mult), ["x_tok", "dtall"], [("xdtb", vA)])
                        needS = not (sA == 2 and lastA)
                        if needS:
                            E("dve", lambda e: e.tensor_tensor(v464(xdd[vA]), xvA, bc_p(wdd[:, cA, hsl]), ALU.mult), ["x_tok", "wdd"], [("xdd", vA)])
                    yv_rest = []
                    if hasB:
                        E("dve", lambda e: e.tensor_tensor(yv[vB], ps[by][:, 0:256], t1[vB], ALU.add), [("ps", by), ("t1", vB)], [("yv", vB)])
                        yv_rest = [lambda: E("dve", lambda e: e.tensor_tensor(yv[vB], yv[vB], t2[vB], ALU.add), [("yv", vB), ("t2", vB)], [("yv", vB)]),
                                   lambda: E("dve", lambda e: e.tensor_tensor(yv[vB], yv[vB], zs[:, cB, :], ALU.mult), [("yv", vB), "zs"], [("yv", vB)])]
                    if not hasA:
                        for op_ in yv_rest:
                            op_()
                        yv_rest = []
                    if hasA:
                        if needS:
                            bS = pbank()
                            E("pe", lambda e: e.matmul(ps[bS][:, 0:256], B_tok[:, cA, :], xdd[vA], start=True, stop=True), ["B_tok", ("xdd", vA)], [("ps", bS)])
                        for dr, Lx in (("f", Lf), ("b", Lb)):
                            b = segb_[dr]
                            E("act", lambda e: e.activation(Lx[vA], ps[b], AF.Exp), [("ps", b)], [("L", dr, vA)])
                        if needS:
                            E("act", lambda e: e.activation(Ssb[ia % 3], ps[bS][:, 0:256], AF.Copy), [("ps", bS)], [("Ssb", ia % 3)])
                        if not firstA or ia > 0:
                            if ia > 0:
                                sP, cP, firstP, lastP = order2[ia - 1]
                                if not firstA or sP < 2:
                                    E("dve", lambda e: e.tensor_tensor(v464(tF), v464(f_state), bc_p(Etab[:, cP, 2, hsl]), ALU.mult), ["f_state"] + ETALL, ["tF"])
                                    if yv_rest:
                                        yv_rest.pop(0)()
                                    E("dve", lambda e: e.tensor_tensor(f_state, tF, Ssb[(ia - 1) % 3], ALU.add), ["tF", ("Ssb", (ia - 1) % 3)], ["f_state"])
                                    if yv_rest:
                                        yv_rest.pop(0)()
                                    if firstA:
                                        store_state(f_state, "f_state", hfo, sP)
                        for op_ in yv_rest:
                            op_()
                        yv_rest = []
                        if firstA:
                            if sA == 2:
                                load_state(h0f_d, f_state, "f_state")
                            else:
                                E("dve", lambda e: e.memset(f_state, 0.0), [], ["f_state"])
                        E("act", lambda e: e.activation(fin[vA], f_state, AF.Copy), ["f_state"], [("fin", vA)])
                        for dr, Lx, MTx in (("f", Lf, MTf), ("b", Lb, MTb)):
                            E("dve", lambda e: e.tensor_tensor(MTx[vA], Lx[vA].rearrange("p (r i) -> p r i", r=4),
                                                               ps[bcb][:, 0:128].unsqueeze(1).broadcast_to([128, 4, 128]), ALU.mult),
                              [("L", dr, vA), ("ps", bcb)], [("MT", dr, vA)])
                    if hasC:
                        E("act", lambda e: e.activation(yaT[:, (g % 2) * 2:(g % 2) * 2 + 2, cslC], psb[bt][:, 0:256].rearrange("p (j d) -> p j d", j=2), AF.Copy),
                          [("ps", bt)], [("yaT", cC)])

                n2 = len(order2)
                for step in range(n2 + 2):
                    p2_step(step)
                if g % 2 == 1:
                    outproj_partial(lambda kt, c: yaT[:, kt, c * 128:(c + 1) * 128], lambda c: [("yaT", c)], 4, wout0, (g - 1) * 256, False, optmp)

            phase()
            dfr = adaln(1, do_phase=False)
            deepnorm_ln(0, do_phase=False)
            for d_ in dfr:
                d_()
            transposes()
            phase()
            V = A.bf(NT * 2048).rearrange("p (c g d) -> p c g d", c=NT, g=16)
            Kt = A.f32(2048).rearrange("p (g i) -> p g i", g=16)
            bsB = A.f32(2048)
            wsTb = A.bf(2048).rearrange("p (g i) -> p g i", g=16)
            ugT = A.f32(NTOK)
            tsA = A.f32(512)
            tsB = A.f32(512)
            s1 = A.f32(48)
            s2 = A.f32(48)
            mean = A.f32(12)
            msq = A.f32(12)
            var = A.f32(12)
            nmr = A.f32(12)
            kb.dma("pool", "wsT", wsTb, wsT_d.rearrange("p (g i) -> p g i", g=16), writes=["wsTb"])
            kb.dma("sp", "bsB", bsB, rowb("cbs"), writes=["bsB"])
            for gq in range(4):
                b = pbank()

                def f(e, gq=gq, b=b):
                    for j in range(4):
                        ins = e.matmul(ps[b][:, j * 128:(j + 1) * 128], cb16["ones"], wsTb[:, gq * 4 + j, :], start=True, stop=True)
                    return ins
                E("pe", f, ["wsTb"] + C16, [("ps", b)])
                for j in range(4):
                    gg = gq * 4 + j
                    E("dve", lambda e, b=b, j=j, gg=gg: e.scalar_tensor_tensor(Kt[:, gg, :], ps[b][:, j * 128:(j + 1) * 128], P("clnb", gg, gg + 1),
                                                                              bsB[:, gg * 128:(gg + 1) * 128], ALU.mult, ALU.add),
                      [("ps", b), "pp", "bsB"], ["Kt"])
            E("dve", lambda e: e.memset(s2, 0.0), [], ["s2"])
            E("dve", lambda e: e.memset(s1, 0.0), [], ["s1"])
            for vb in range(4):
                slot, wn = load_w([(win1[:, 2048 + vb * 512:2048 + (vb + 1) * 512], 0, 0)])
                for c in range(NT):
                    b = pbank()

                    def f(e, slot=slot, c=c, b=b):
                        for k in range(8):
                            ins = e.matmul(ps[b], hT[:, k, c * 128:(c + 1) * 128], slot[:, k, :], start=(k == 0), stop=(k == 7))
                        return ins
                    E("pe", f, wn + [("hT", c)], [("ps", b)])
                    E("act", lambda e, c=c, vb=vb, b=b: e.activation(V[:, c, vb * 4:(vb + 1) * 4, :].rearrange("p g d -> p (g d)"), ps[b], AF.Copy,
                                                                     accum_out=s1[:, c * 4 + vb:c * 4 + vb + 1]),
                      [("ps", b), "s1"], [("V", c), ("s1", c, vb)])
                    E("act", lambda e, c=c, vb=vb, b=b: e.activation(tsA, ps[b], AF.Square, accum_out=s2[:, c * 4 + vb:c * 4 + vb + 1]),
                      [("ps", b), "s2"], ["tsAj", ("s2", c, vb)])
            SALL = [("s1", c, vb) for c in range(NT) for vb in range(4)] + [("s2", c, vb) for c in range(NT) for vb in range(4)]
            E("dve", lambda e: e.reduce_sum(mean, s1.rearrange("p (c v) -> p c v", v=4), axis=AX.X), SALL, ["mean"])
            E("dve", lambda e: e.reduce_sum(var, s2.rearrange("p (c v) -> p c v", v=4), axis=AX.X), SALL, ["var"])
            E("dve", lambda e: e.tensor_scalar(mean, mean, 1.0 / 2048, None, ALU.mult), ["mean"], ["mean"])
            E("dve", lambda e: e.tensor_tensor(msq, mean, mean, ALU.mult), ["mean"], ["msq"])
            E("dve", lambda e: e.scalar_tensor_tensor(var, var, 1.0 / 2048, msq, ALU.mult, ALU.subtract), ["var", "msq"], ["var"])
            E("act", lambda e: e.activation(var, var, AF.Ln, bias=pp_eps_ln), ["var", "eps"], ["var"])
            E("act", lambda e: e.activation(var, var, AF.Exp, scale=-0.5), ["var"], ["var"])
            E("dve", lambda e: e.scalar_tensor_tensor(nmr, mean, -1.0, var, ALU.mult, ALU.mult), ["mean", "var"], ["nmr"])
            for c in range(NT):
                vc = V[:, c, :, :].rearrange("p g d -> p (g d)")
                E("dve", lambda e, c=c, vc=vc: e.tensor_scalar(vc, vc, var[:, c:c + 1], nmr[:, c:c + 1], ALU.mult, ALU.add),
                  [("V", c), "var", "nmr"], [("V", c)])
            for g in range(16):
                slot, wn = load_w([(win1[:, g * 128:(g + 1) * 128], 0, 0), (win1[:, 4096 + g * 128:4096 + (g + 1) * 128], 0, 128)])
                for tb in range(3):
                    bu = pbank()
                    bg = pbank()
                    for j, b in ((0, bu), (1, bg)):
                        def f(e, slot=slot, j=j, tb=tb, b=b):
                            for k in range(8):
                                ins = e.matmul(ps[b], slot[:, k, j * 128:(j + 1) * 128], hT[:, k, tb * 512:(tb + 1) * 512], start=(k == 0), stop=(k == 7))
                            return ins
                        E("pe", f, wn + hT_of_tb(tb), [("ps", b)])
                    E("act", lambda e, bg=bg: e.activation(tsA, ps[bg], AF.Silu), [("ps", bg)], ["tsAj"])
                    E("dve", lambda e, bu=bu, tb=tb: e.tensor_tensor(ugT[:, tb * 512:(tb + 1) * 512], ps[bu], tsA, ALU.mult), [("ps", bu), "tsAj"], [("ug", tb)])
                for cq in range(3):
                    b = pbank()

                    def f(e, cq=cq, g=g, b=b):
                        for j in range(4):
                            ins = e.matmul(ps[b][:, j * 128:(j + 1) * 128], V[:, cq * 4 + j, g, :], wsTb[:, g, :], start=True, stop=True)
                        return ins
                    E("pe", f, [("V", cq * 4 + j) for j in range(4)] + ["wsTb"], [("ps", b)])
                    E("dve", lambda e, b=b, g=g: e.scalar_tensor_tensor(tsB.rearrange("p (j i) -> p j i", j=4), ps[b].rearrange("p (j i) -> p j i", j=4),
                                                                        P("clng", g, g + 1), Kt[:, g, :].unsqueeze(1).broadcast_to([128, 4, 128]),
                                                                        ALU.mult, ALU.add),
                      [("ps", b), "pp", "Kt"], ["tsB"])
                    E("dve", lambda e, cq=cq, g=g: e.tensor_tensor(V[:, cq * 4:cq * 4 + 4, g, :], tsB.rearrange("p (j i) -> p j i", j=4),
                                                                   ugT[:, cq * 512:(cq + 1) * 512].rearrange("p (j i) -> p j i", j=4), ALU.mult),
                      ["tsB", ("ug", cq)], [("V", cq * 4 + j) for j in range(4)])
            outproj_partial(lambda kt, c: V[:, c, kt, :], lambda c: [("V", c)], 16, wout1, 0, True, [(tsB, "tsB"), (tsA, "tsAj"), (ugT[:, 0:512], ("ug", 0))])
            deepnorm_ln(1)
            for t in range(NT):
                kb.dma("sp", "y%d" % t, yout[t * 128:(t + 1) * 128, :], resid[:, t, :], reads=[("res", t)])
        try:
            _body()
        except _Stop:
            kb.barrier()
            for t in range(NT):
                kb.dma("sp", "y%d" % t, yout[t * 128:(t + 1) * 128, :], resid[:, t, :], reads=[("res", t)])
        kb.barrier()
        print("arena peak words", A.peak, "instr counts", kb.cnt, "dma sems", len(kb.dsem))
        kb.replay()
    return nc


def _host_pp(core, inp):
    pp = np.zeros((128, NPP), np.float32)
    p = np.arange(128)

    def put(name, arr):
        o, n = PPL[name]
        pp[:, o:o + n] = np.asarray(arr, np.float32).reshape(128, n)
    put("ident", np.eye(128))
    t = p[:, None]
    i = p[None, :]
    put("tle", (t <= i))
    put("tgt", (t > i))
    put("tge", (t >= i))
    put("tlt", (t < i))
    put("ones", np.ones((128, 128)))
    put("pcol", (p % 64)[:, None])
    put("rowv", (2 * np.arange(8)[None, :] + (p // 64)[:, None]))
    cond = np.stack([inp["c_ctx"], inp["c"][core]], 0)
    put("condT", cond.reshape(2, 8, 128).transpose(2, 1, 0).reshape(128, 16))
    put("adab0T", inp["ada_b_l0"][:2048].reshape(16, 128).T)
    put("adab1T", inp["ada_b_l1"][:2048].reshape(16, 128).T)
    put("aconvw", inp["a_conv_w_l0"].reshape(4, 16, 128).transpose(2, 1, 0).reshape(128, 64))
    put("aconvb", inp["a_conv_b_l0"].reshape(16, 128).T)
    put("bconvw", inp["b_conv_w_l0"].reshape(31, 8, 128).transpose(2, 1, 0).reshape(128, 248))
    put("bconvb", inp["b_conv_b_l0"].reshape(8, 128).T)
    put("blng", inp["b_ln_g_l0"].reshape(8, 128).T)
    put("blnb", inp["b_ln_b_l0"].reshape(8, 128).T)
    put("clng", inp["c_ln_g_l1"].reshape(16, 128).T)
    put("clnb", inp["c_ln_b_l1"].reshape(16, 128).T)
    return pp


def _host_rows(inp):
    rows = np.zeros((1, NRW), np.float32)

    def put(name, arr):
        o, n = RWL[name]
        rows[0, o:o + n] = np.asarray(arr, np.float32).reshape(n)
    put("qidx", np.arange(256))
    put("adab0g", inp["ada_b_l0"][2048:])
    put("adab1g", inp["ada_b_l1"][2048:])
    put("lng0", inp["ln_g_l0"])
    put("lnb0", inp["ln_b_l0"])
    put("lng1", inp["ln_g_l1"])
    put("lnb1", inp["ln_b_l1"])
    put("anormw", inp["a_norm_w_l0"])
    put("dtbias", np.concatenate([inp["a_dt_bias_f_l0"], inp["a_dt_bias_b_l0"]]))
    put("alog", np.concatenate([inp["a_log_f_l0"], inp["a_log_b_l0"]]))
    put("ad", inp["a_d_l0"])
    put("cbs", inp["c_bs_l1"])
    return rows


_NC_CACHE = {}


def kernel(**inputs):
    inp = {k: np.asarray(v) for k, v in inputs.items()}
    if "nc" not in _NC_CACHE:
        _NC_CACHE["nc"] = build_nc()
    nc = _NC_CACHE["nc"]
    rows = _host_rows(inp)
    wsT = np.ascontiguousarray(inp["c_ws_l1"].transpose(2, 0, 1).reshape(128, 2048)).astype(np.float32)
    shared = {
        "rows": rows, "wsT": wsT,
        "ada0": np.ascontiguousarray(inp["ada_w_l0"], np.float32), "ada1": np.ascontiguousarray(inp["ada_w_l1"], np.float32),
        "win0": np.ascontiguousarray(inp["w_in_l0"], np.float32), "wout0": np.ascontiguousarray(inp["w_out_l0"], np.float32),
        "win1": np.ascontiguousarray(inp["w_in_l1"], np.float32), "wout1": np.ascontiguousarray(inp["w_out_l1"], np.float32),
    }
    in_maps = []
    for c in range(8):
        xin = np.concatenate([inp["x_prompt"][2 * c], inp["x_prompt"][2 * c + 1], inp["x_sample"][c]], 0).astype(np.float32)
        m = dict(shared)
        m["xin"] = np.ascontiguousarray(xin)
        m["pp"] = _host_pp(c, inp)
        m["h0f"] = np.ascontiguousarray(inp["state_ssd_fwd_l0"][c].reshape(1024, 128), np.float32)
        m["h0b"] = np.ascontiguousarray(inp["state_ssd_bwd_l0"][c].reshape(1024, 128), np.float32)
        in_maps.append(m)
    res = run_bass_kernel_spmd(nc, in_maps, core_ids=list(range(8)))
    y_p = np.zeros((16, 256, D), np.float32)
    y_s = np.zeros((8, 1024, D), np.float32)
    hf = np.zeros((16, 16, 64, 128), np.float32)
    hb = np.zeros((16, 16, 64, 128), np.float32)
    for c in range(8):
        r = res.results[c]
        y = r["yout"]
        y_p[2 * c] = y[0:256]
        y_p[2 * c + 1] = y[256:512]
        y_s[c] = y[512:1536]
        hf[2 * c:2 * c + 2] = r["hfo"].reshape(2, 16, 64, 128)
        hb[2 * c:2 * c + 2] = r["hbo"].reshape(2, 16, 64, 128)
    return (y_p, y_s, hf, hb)
```
